# Optimizing a Trainium2 kernel written in Bass

```python
import jax
import jax.numpy as jnp
from jax import lax
import numpy as np

D_MODEL = 1024
BATCH = 16
SEQ = 2048
DEPTH = 2

GRID_W = 64
CTX_LEN = 256
HEAD_DIM = 64
ATT_HEADS = 8
ATT_KV_HEADS = 2
ATT_GROUPS = ATT_HEADS // ATT_KV_HEADS
ATT_WIDTH = ATT_HEADS * HEAD_DIM
KV_WIDTH = ATT_KV_HEADS * HEAD_DIM
WINDOW = 128
BLOCK = 128
ATT_SCALE = HEAD_DIM ** -0.5
ROPE_THETA = 10000.0
ROPE_FREQS = HEAD_DIM // 4
POOL_WINDOWS = (2, 4, 8, 16)
POOL_GROUPS = len(POOL_WINDOWS)
POOL_WIDTH = D_MODEL // 2
POOL_GROUP_W = POOL_WIDTH // POOL_GROUPS
MIX_AB_IN = ATT_WIDTH + 2 * KV_WIDTH + POOL_WIDTH
MIX_AB_OUT = ATT_WIDTH + POOL_WIDTH
LRU_WIDTH = D_MODEL
LRU_BLOCKS = 8
LRU_BLOCK_W = LRU_WIDTH // LRU_BLOCKS
LRU_C = 8.0
CONV_W = 4
CONV_LEFT = (CONV_W - 1) // 2
D_FF = 2816
N_MOD = 9
LN_EPS = 1e-5
NEG_INF = -1e30
DEEPNORM_ALPHA = (2 * DEPTH) ** 0.25
DEEPNORM_BETA = (8 * DEPTH) ** -0.25
N_EVEN = (DEPTH + 1) // 2
N_ODD = DEPTH // 2

kernel_name = 'hybrid_window_attn_pool_rglru_macaron_dit'


def layer_norm(x, g, b):
    xf = x.astype(jnp.float32)
    mu = jnp.mean(xf, axis=-1, keepdims=True)
    var = jnp.mean(jnp.square(xf - mu), axis=-1, keepdims=True)
    return ((xf - mu) * lax.rsqrt(var + LN_EPS)).astype(x.dtype) * g + b


def residual_post_norm(x, y, g, b):
    return layer_norm(DEEPNORM_ALPHA * x + y, g, b)


def modulate(x, shift, scale):
    return x * (1.0 + scale) + shift


def swiglu(x, w_gate, w_up, w_down):
    return (jax.nn.silu(x @ w_gate) * (x @ w_up)) @ w_down


def axial_rope(rows):
    row = jnp.repeat(jnp.arange(rows, dtype=jnp.float32), GRID_W)
    col = jnp.tile(jnp.arange(GRID_W, dtype=jnp.float32), rows)
    inv = ROPE_THETA ** (-jnp.arange(ROPE_FREQS, dtype=jnp.float32) / ROPE_FREQS)
    ang = jnp.concatenate([row[:, None] * inv, col[:, None] * inv], axis=-1)
    return jnp.cos(ang), jnp.sin(ang)


def apply_rope(x, cos, sin):
    half = HEAD_DIM // 2
    cs = cos[None, :, None, :].astype(x.dtype)
    sn = sin[None, :, None, :].astype(x.dtype)
    x1, x2 = x[..., :half], x[..., half:]
    return jnp.concatenate([x1 * cs - x2 * sn, x2 * cs + x1 * sn], axis=-1)


def windowed_sink_attention(q, k, v, k_ctx, v_ctx, sink):
    bsz, n = q.shape[0], q.shape[1]
    nb = n // BLOCK
    qb = q.reshape(bsz, nb, BLOCK, ATT_KV_HEADS, ATT_GROUPS, HEAD_DIM)

    def band(t):
        tp = jnp.pad(t, ((0, 0), (BLOCK, BLOCK), (0, 0), (0, 0)))
        tp = tp.reshape(bsz, nb + 2, BLOCK, ATT_KV_HEADS, HEAD_DIM)
        return jnp.concatenate([tp[:, :-2], tp[:, 1:-1], tp[:, 2:]], axis=2)

    kb, vb = band(k), band(v)
    s_win = jnp.einsum('bnqkgd,bnskd->bnkgqs', qb, kb, preferred_element_type=jnp.float32) * ATT_SCALE
    qpos = jnp.arange(n).reshape(nb, BLOCK)
    kpos = (jnp.arange(nb)[:, None] - 1) * BLOCK + jnp.arange(3 * BLOCK)[None, :]
    valid = ((jnp.abs(qpos[:, :, None] - kpos[:, None, :]) <= WINDOW)
             & (kpos[:, None, :] >= 0) & (kpos[:, None, :] < n))
    s_win = jnp.where(valid[None, :, None, None], s_win, NEG_INF)
    s_ctx = jnp.einsum('bnqkgd,bckd->bnkgqc', qb, k_ctx, preferred_element_type=jnp.float32) * ATT_SCALE
    sink_l = sink.astype(jnp.float32).reshape(ATT_KV_HEADS, ATT_GROUPS)[None, None, :, :, None, None]
    m = jnp.maximum(jnp.maximum(s_win.max(-1, keepdims=True), s_ctx.max(-1, keepdims=True)), sink_l)
    e_win = jnp.exp(s_win - m)
    e_ctx = jnp.exp(s_ctx - m)
    denom = e_win.sum(-1, keepdims=True) + e_ctx.sum(-1, keepdims=True) + jnp.exp(sink_l - m)
    o = (jnp.einsum('bnkgqs,bnskd->bnqkgd', (e_win / denom).astype(v.dtype), vb)
         + jnp.einsum('bnkgqc,bckd->bnqkgd', (e_ctx / denom).astype(v.dtype), v_ctx))
    return o.reshape(bsz, n, ATT_WIDTH)


def context_sink_attention(q_ctx, k_ctx, v_ctx, sink):
    bsz, n_c = q_ctx.shape[0], q_ctx.shape[1]
    qg = q_ctx.reshape(bsz, n_c, ATT_KV_HEADS, ATT_GROUPS, HEAD_DIM)
    s = jnp.einsum('bqkgd,bskd->bkgqs', qg, k_ctx, preferred_element_type=jnp.float32) * ATT_SCALE
    sink_c = jnp.broadcast_to(sink.astype(jnp.float32).reshape(ATT_KV_HEADS, ATT_GROUPS)[None, :, :, None, None],
                              s.shape[:-1] + (1,))
    p = jax.nn.softmax(jnp.concatenate([s, sink_c], axis=-1), axis=-1)[..., :n_c]
    o = jnp.einsum('bkgqs,bskd->bqkgd', p.astype(v_ctx.dtype), v_ctx)
    return o.reshape(bsz, n_c, ATT_WIDTH)


def multiscale_pool(u, w_pool, pool_scale):
    bsz, n = u.shape[0], u.shape[1]
    uf = u.astype(jnp.float32)
    cs = jnp.pad(jnp.cumsum(uf, axis=1), ((0, 0), (1, 0), (0, 0)))
    t = jnp.arange(n)
    diffs = []
    for g, w in enumerate(POOL_WINDOWS):
        r = w // 2
        lo = jnp.maximum(t - r, 0)
        hi = jnp.minimum(t + r, n - 1) + 1
        sl = slice(g * POOL_GROUP_W, (g + 1) * POOL_GROUP_W)
        seg = cs[:, :, sl]
        mean = (seg[:, hi] - seg[:, lo]) / (hi - lo).astype(jnp.float32)[None, :, None]
        diffs.append(mean - uf[:, :, sl])
    d = jnp.stack(diffs, axis=2).astype(u.dtype)
    y = jnp.einsum('blgc,gce->blge', d, w_pool).reshape(bsz, n, POOL_WIDTH)
    return y * pool_scale


def attention_pool_mixer(h, hc, cos, sin, w_in, sink, w_pool, pool_scale, w_out, need_ctx_out):
    bsz, n = h.shape[0], h.shape[1]
    n_c = hc.shape[1]
    splits = [ATT_WIDTH, ATT_WIDTH + KV_WIDTH, ATT_WIDTH + 2 * KV_WIDTH]
    q, k, v, u = jnp.split(h @ w_in, splits, axis=-1)
    if need_ctx_out:
        q_c, k_c, v_c, u_c = jnp.split(hc @ w_in, splits, axis=-1)
    else:
        k_c, v_c = jnp.split(hc @ w_in[:, ATT_WIDTH:ATT_WIDTH + 2 * KV_WIDTH], 2, axis=-1)
    q = apply_rope(q.reshape(bsz, n, ATT_HEADS, HEAD_DIM), cos, sin)
    k = apply_rope(k.reshape(bsz, n, ATT_KV_HEADS, HEAD_DIM), cos, sin)
    v = v.reshape(bsz, n, ATT_KV_HEADS, HEAD_DIM)
    k_c = k_c.reshape(bsz, n_c, ATT_KV_HEADS, HEAD_DIM)
    v_c = v_c.reshape(bsz, n_c, ATT_KV_HEADS, HEAD_DIM)
    att = windowed_sink_attention(q, k, v, k_c, v_c, sink)
    pool = multiscale_pool(u, w_pool, pool_scale)
    out = jnp.concatenate([att, pool], axis=-1) @ w_out
    if not need_ctx_out:
        return out, None
    att_c = context_sink_attention(q_c, k_c, v_c, sink)
    pool_c = multiscale_pool(u_c, w_pool, pool_scale)
    out_c = jnp.concatenate([att_c, pool_c], axis=-1) @ w_out
    return out, out_c


def centred_depthwise_conv(x, w, b):
    n = x.shape[1]
    xp = jnp.pad(x, ((0, 0), (CONV_LEFT, CONV_W - 1 - CONV_LEFT), (0, 0)))
    y = b
    for tap in range(CONV_W):
        y = y + xp[:, tap:tap + n] * w[tap]
    return y


def rglru_coeffs(u, wa, ba, wx, bx, lam):
    bsz, n = u.shape[0], u.shape[1]
    ub = u.reshape(bsz, n, LRU_BLOCKS, LRU_BLOCK_W)
    r = jax.nn.sigmoid(jnp.einsum('blhi,hij->blhj', ub, wa).reshape(bsz, n, LRU_WIDTH) + ba)
    gi = jax.nn.sigmoid(jnp.einsum('blhi,hij->blhj', ub, wx).reshape(bsz, n, LRU_WIDTH) + bx)
    log_a = -LRU_C * r.astype(jnp.float32) * jax.nn.softplus(-lam.astype(jnp.float32))
    a = jnp.exp(log_a)
    b = jnp.sqrt(-jnp.expm1(2.0 * log_a)) * (gi * u).astype(jnp.float32)
    return a, b


def linear_scan(a, b, h0):
    b = b.at[:, 0].add(a[:, 0] * h0)

    def combine(left, right):
        return left[0] * right[0], right[0] * left[1] + right[1]

    _, h = lax.associative_scan(combine, (a, b), axis=1)
    return h


def recurrent_mixer(h, hc, w_in, conv_w, conv_b, wa, ba, wx, bx, lam, w_out, need_ctx_out):
    gate, u = jnp.split(h @ w_in, 2, axis=-1)
    if need_ctx_out:
        gate_c, u_c = jnp.split(hc @ w_in, 2, axis=-1)
    else:
        u_c = hc @ w_in[:, LRU_WIDTH:]
    u = centred_depthwise_conv(u, conv_w, conv_b)
    u_c = centred_depthwise_conv(u_c, conv_w, conv_b)
    h0 = jnp.zeros((h.shape[0], LRU_WIDTH), jnp.float32)
    ys, ys_c = [], []
    for direction in range(2):
        a, b = rglru_coeffs(u, wa[direction], ba[direction], wx[direction], bx[direction], lam[direction])
        a_c, b_c = rglru_coeffs(u_c, wa[direction], ba[direction], wx[direction], bx[direction], lam[direction])
        if direction == 1:
            a, b, a_c, b_c = (jnp.flip(a, 1), jnp.flip(b, 1), jnp.flip(a_c, 1), jnp.flip(b_c, 1))
        s_c = linear_scan(a_c, b_c, h0)
        s = linear_scan(a, b, s_c[:, -1])
        if direction == 1:
            s, s_c = jnp.flip(s, 1), jnp.flip(s_c, 1)
        ys.append(s)
        ys_c.append(s_c)
    y = (ys[0] + ys[1]).astype(h.dtype)
    out = (jax.nn.gelu(gate) * y) @ w_out
    if not need_ctx_out:
        return out, None
    y_c = (ys_c[0] + ys_c[1]).astype(hc.dtype)
    out_c = (jax.nn.gelu(gate_c) * y_c) @ w_out
    return out, out_c


def setup_inputs(seed: int = 0) -> dict:
    key = jax.random.key(seed)
    ks = jax.random.split(key, 26)

    def nrm(i, shape, scale):
        return jax.random.normal(ks[i], shape, jnp.float32) * scale

    lam_u = jax.random.uniform(ks[23], (N_ODD, 2, LRU_WIDTH), jnp.float32, 0.9, 0.999)
    return {
        'x': nrm(0, (BATCH, SEQ, D_MODEL), 1.0),
        'c': nrm(1, (BATCH, D_MODEL), 1.0),
        'ctx': nrm(2, (BATCH, CTX_LEN, D_MODEL), 1.0),
        'c_ctx': nrm(3, (D_MODEL,), 1.0),
        'w_mod': nrm(4, (DEPTH, D_MODEL, N_MOD * D_MODEL), 0.5 * D_MODEL ** -0.5),
        'b_mod': nrm(5, (DEPTH, N_MOD * D_MODEL), 0.02),
        'ln_g': 1.0 + nrm(6, (DEPTH, 3, D_MODEL), 0.02),
        'ln_b': nrm(7, (DEPTH, 3, D_MODEL), 0.02),
        'ffn_w_gate': nrm(8, (DEPTH, 2, D_MODEL, D_FF), D_MODEL ** -0.5),
        'ffn_w_up': nrm(9, (DEPTH, 2, D_MODEL, D_FF), D_MODEL ** -0.5),
        'ffn_w_down': nrm(10, (DEPTH, 2, D_FF, D_MODEL), DEEPNORM_BETA * D_FF ** -0.5),
        'mix_ab_w_in': nrm(11, (N_EVEN, D_MODEL, MIX_AB_IN), D_MODEL ** -0.5),
        'attn_sink': nrm(12, (N_EVEN, ATT_HEADS), 0.5),
        'pool_w': nrm(13, (N_EVEN, POOL_GROUPS, POOL_GROUP_W, POOL_GROUP_W), POOL_GROUP_W ** -0.5),
        'pool_scale': 1.0 + nrm(14, (N_EVEN, POOL_WIDTH), 0.1),
        'mix_ab_w_out': nrm(15, (N_EVEN, MIX_AB_OUT, D_MODEL), DEEPNORM_BETA * MIX_AB_OUT ** -0.5),
        'lru_w_in': nrm(16, (N_ODD, D_MODEL, 2 * LRU_WIDTH), D_MODEL ** -0.5),
        'lru_conv_w': nrm(17, (N_ODD, CONV_W, LRU_WIDTH), CONV_W ** -0.5),
        'lru_conv_b': nrm(18, (N_ODD, LRU_WIDTH), 0.02),
        'lru_wa': nrm(19, (N_ODD, 2, LRU_BLOCKS, LRU_BLOCK_W, LRU_BLOCK_W), LRU_BLOCK_W ** -0.5),
        'lru_ba': nrm(20, (N_ODD, 2, LRU_WIDTH), 0.02),
        'lru_wx': nrm(21, (N_ODD, 2, LRU_BLOCKS, LRU_BLOCK_W, LRU_BLOCK_W), LRU_BLOCK_W ** -0.5),
        'lru_bx': nrm(22, (N_ODD, 2, LRU_WIDTH), 0.02),
        'lru_lambda': jnp.log(lam_u) - jnp.log1p(-lam_u),
        'lru_w_out': nrm(24, (N_ODD, LRU_WIDTH, D_MODEL), DEEPNORM_BETA * LRU_WIDTH ** -0.5),
    }


def reference(x, c, ctx, c_ctx, w_mod, b_mod, ln_g, ln_b, ffn_w_gate, ffn_w_up, ffn_w_down,
              mix_ab_w_in, attn_sink, pool_w, pool_scale, mix_ab_w_out,
              lru_w_in, lru_conv_w, lru_conv_b, lru_wa, lru_ba, lru_wx, lru_bx, lru_lambda, lru_w_out):
    rows = x.shape[1] // GRID_W
    cos, sin = axial_rope(rows)
    h, hc = x, ctx
    for layer in range(DEPTH):
        ctx_out = layer < DEPTH - 1
        m = jnp.split((jax.nn.silu(c) @ w_mod[layer] + b_mod[layer])[:, None, :], N_MOD, axis=-1)
        mc = jnp.split((jax.nn.silu(c_ctx) @ w_mod[layer] + b_mod[layer])[None, None, :], N_MOD, axis=-1)
        ffn1 = (ffn_w_gate[layer, 0], ffn_w_up[layer, 0], ffn_w_down[layer, 0])
        ffn2 = (ffn_w_gate[layer, 1], ffn_w_up[layer, 1], ffn_w_down[layer, 1])
        h = residual_post_norm(h, 0.5 * m[2] * swiglu(modulate(h, m[0], m[1]), *ffn1), ln_g[layer, 0], ln_b[layer, 0])
        hc = residual_post_norm(hc, 0.5 * mc[2] * swiglu(modulate(hc, mc[0], mc[1]), *ffn1), ln_g[layer, 0], ln_b[layer, 0])
        h_in = modulate(h, m[3], m[4])
        hc_in = modulate(hc, mc[3], mc[4])
        idx = layer // 2
        if layer % 2 == 0:
            y, y_c = attention_pool_mixer(h_in, hc_in, cos, sin, mix_ab_w_in[idx], attn_sink[idx],
                                          pool_w[idx], pool_scale[idx], mix_ab_w_out[idx], ctx_out)
        else:
            y, y_c = recurrent_mixer(h_in, hc_in, lru_w_in[idx], lru_conv_w[idx], lru_conv_b[idx],
                                     lru_wa[idx], lru_ba[idx], lru_wx[idx], lru_bx[idx], lru_lambda[idx],
                                     lru_w_out[idx], ctx_out)
        h = residual_post_norm(h, m[5] * y, ln_g[layer, 1], ln_b[layer, 1])
        h = residual_post_norm(h, 0.5 * m[8] * swiglu(modulate(h, m[6], m[7]), *ffn2), ln_g[layer, 2], ln_b[layer, 2])
        if ctx_out:
            hc = residual_post_norm(hc, mc[5] * y_c, ln_g[layer, 1], ln_b[layer, 1])
            hc = residual_post_norm(hc, 0.5 * mc[8] * swiglu(modulate(hc, mc[6], mc[7]), *ffn2), ln_g[layer, 2], ln_b[layer, 2])
    return h
```

```python
import contextlib
import math
import numpy as np
import ml_dtypes
import concourse.bass as bass
import concourse.mybir as mybir
from concourse.bass_utils import run_bass_kernel_spmd

F32 = mybir.dt.float32
BF = mybir.dt.bfloat16
AF = mybir.ActivationFunctionType
ALU = mybir.AluOpType

ENGINES = ("pe", "act", "dve", "pool", "sp")
GRAN = 256
_DTSIZE = {}


def _dtsize(dt):
    s = _DTSIZE.get(dt)
    if s is None:
        name = str(dt)
        s = 4 if "32" in name else (2 if "16" in name else (8 if "64" in name else 1))
        _DTSIZE[dt] = s
    return s


def ap_granules(ap):
    t = ap.tensor
    name = t.name
    pat = ap.ap
    esz = _dtsize(ap.dtype)
    pstride = pat[0][0]
    off = ap.offset
    if pstride > 0:
        off = off % pstride
    dims = [(s, c) for (s, c) in pat[1:] if c > 1]
    if not dims:
        return name, {(off * esz) // GRAN}
    s_in, c_in = dims[-1]
    outer = dims[:-1]
    nouter = 1
    for _, c in outer:
        nouter *= c
    gr = set()
    if nouter > 256 or abs(s_in) > 1:
        lo = hi = off
        for s, c in dims:
            if s >= 0:
                hi += s * (c - 1)
            else:
                lo += s * (c - 1)
        gr.update(range((lo * esz) // GRAN, ((hi + 1) * esz - 1) // GRAN + 1))
        return name, gr
    starts = [off]
    for s, c in outer:
        starts = [b + s * i for b in starts for i in range(c)]
    for b in starts:
        if s_in >= 0:
            lo, hi = b, b + s_in * (c_in - 1)
        else:
            lo, hi = b + s_in * (c_in - 1), b
        gr.update(range((lo * esz) // GRAN, ((hi + 1) * esz - 1) // GRAN + 1))
    return name, gr


class Instr:
    __slots__ = ("eng", "fn", "deps", "signal", "value", "is_dma", "sem_key", "idx", "dma_value")

    def __init__(self, eng, fn, is_dma=False, sem_key=None):
        self.eng = eng
        self.fn = fn
        self.deps = set()
        self.signal = False
        self.value = None
        self.is_dma = is_dma
        self.sem_key = sem_key
        self.idx = None
        self.dma_value = None


class Sched:
    def __init__(self, nc):
        self.nc = nc
        self.streams = {e: [] for e in ENGINES}
        self.writer = {}
        self.readers = {}
        self.dma_counts = {}
        self.n_instr = 0

    def _add_dep(self, ins, prod, raw):
        if prod is None or prod is ins:
            return
        if prod.is_dma:
            ins.deps.add(prod)
            return
        if prod.eng == ins.eng and not ins.is_dma:
            if prod.eng == "pe":
                return
            if not raw:
                return
            if prod.eng != "pool" and ins.idx - prod.idx > 2:
                return
        prod.signal = True
        ins.deps.add(prod)

    def _keys(self, items, excl=None):
        out = []
        for it in items:
            if isinstance(it, (str, tuple)):
                out.append(it)
            else:
                name = it.tensor.name
                if name.startswith("pb"):
                    (excl if excl is not None else out).append((name, "bank"))
                    continue
                name, gr = ap_granules(it)
                out.extend((name, g) for g in gr)
        return out

    def op(self, eng, fn, reads=(), writes=(), sem_key=None):
        is_dma = sem_key is not None
        ins = Instr(eng, fn, is_dma=is_dma, sem_key=sem_key)
        ins.idx = len(self.streams[eng])
        wk = self._keys(writes)
        rk = self._keys(reads, excl=wk)
        W = self.writer
        RD = self.readers
        for b in rk:
            self._add_dep(ins, W.get(b), True)
        for b in wk:
            self._add_dep(ins, W.get(b), False)
            rd = RD.get(b)
            if rd:
                for r in rd.values():
                    self._add_dep(ins, r, False)
        for b in rk:
            rd = RD.get(b)
            if rd is None:
                rd = RD[b] = {}
            if is_dma:
                rd[("dma", id(ins))] = ins
            else:
                rd[eng] = ins
        for b in wk:
            W[b] = ins
            RD[b] = {}
        if is_dma:
            v = self.dma_counts.get(sem_key, 0) + 16
            self.dma_counts[sem_key] = v
            ins.dma_value = v
        self.streams[eng].append(ins)
        self.n_instr += 1
        if not is_dma:
            self.last = ins
        return ins

    def mark(self, key):
        self.writer[key] = self.last
        self.readers[key] = {}

    def emit(self, final_dma_waits=()):
        nc = self.nc
        for e in ENGINES:
            c = 0
            for ins in self.streams[e]:
                if ins.is_dma:
                    continue
                if ins.signal:
                    c += 1
                ins.value = c
        with contextlib.ExitStack() as st:
            esem = {e: st.enter_context(nc.semaphore("prog_" + e)) for e in ENGINES}
            dsem = {k: st.enter_context(nc.semaphore("dma_%d" % i)) for i, k in enumerate(self.dma_counts)}
            block = st.enter_context(nc.Block())

            def run_stream(e, eng):
                waited = {}
                for ins in self.streams[e]:
                    need = {}
                    for p in ins.deps:
                        if p.is_dma:
                            key = ("d", p.sem_key)
                            val = p.dma_value
                        else:
                            key = ("e", p.eng)
                            val = p.value
                        if val > need.get(key, 0):
                            need[key] = val
                    for key, val in need.items():
                        if waited.get(key, 0) >= val:
                            continue
                        waited[key] = val
                        sem = dsem[key[1]] if key[0] == "d" else esem[key[1]]
                        eng.wait_ge(sem, val)
                    r = ins.fn(eng)
                    if ins.is_dma:
                        r.then_inc(dsem[ins.sem_key], 16)
                    elif ins.signal:
                        r.then_inc(esem[e], 1)
                for p in final_dma_waits:
                    if p.eng == e:
                        eng.wait_ge(dsem[p.sem_key], p.dma_value)

            @block.sync
            def _(eng):
                run_stream("sp", eng)

            @block.scalar
            def _(eng):
                run_stream("act", eng)

            @block.vector
            def _(eng):
                run_stream("dve", eng)

            @block.gpsimd
            def _(eng):
                run_stream("pool", eng)

            @block.tensor
            def _(eng):
                run_stream("pe", eng)


D = 1024
NCH = 8
SEQ = 2048
CTX = 256
L = CTX + SEQ
DFF = 2816
NJ = 22
NB = 16
NCORES = 8
SPC = NB // NCORES
ALPHA = 4.0 ** 0.25
EPS_P = 1e-5 / (ALPHA * ALPHA)
NEG = -30000.0
ARENA_BYTES = 212736
import os as _os
_ABP = int(_os.environ.get("ABP", "4"))
_ABQ = int(_os.environ.get("ABQ", "9"))

TILES = [(0, CTX, 2)] + [(CTX + 512 * i, 512, None) for i in range(4)]


def _host_consts():
    rows = SEQ // 64
    t = np.arange(SEQ)
    row = (t // 64).astype(np.float32)
    col = (t % 64).astype(np.float32)
    inv = (10000.0 ** (-np.arange(16, dtype=np.float32) / 16)).astype(np.float32)
    ang = np.concatenate([row[:, None] * inv, col[:, None] * inv], axis=-1).astype(np.float32)
    cos = np.cos(ang).astype(np.float32)
    sin = np.sin(ang).astype(np.float32)
    cs = np.zeros((128, 2, SEQ), np.float32)
    for p in range(128):
        d = p % 64
        j = d % 32
        cs[p, 0] = cos[:, j]
        cs[p, 1] = (-sin[:, j]) if d < 32 else sin[:, j]
    i = np.arange(128)[:, None]
    j = np.arange(128)[None, :]
    m_prev = np.where(j <= i, 0.0, NEG).astype(np.float32)
    m_next = np.where(i <= j, 0.0, NEG).astype(np.float32)
    masks = np.stack([np.tile(m_prev, (1, 4)), np.tile(m_next, (1, 4))], axis=1)
    pm = np.zeros((128, 4, 5, 128), np.float32)
    for g, w in enumerate((2, 4, 8, 16)):
        r = w // 2
        n = 3 * 128
        full_mid = np.zeros((n, n), np.float64)
        for tt in range(n):
            lo, hi = max(tt - r, 0), min(tt + r, n - 1)
            full_mid[lo:hi + 1, tt] = 1.0 / (hi - lo + 1)
            full_mid[tt, tt] -= 1.0
        pm[:, g, 0] = full_mid[0:128, 128:256]
        pm[:, g, 2] = full_mid[128:256, 128:256]
        pm[:, g, 4] = full_mid[256:384, 128:256]
        pm[:, g, 1] = full_mid[0:128, 0:128]
        pm[:, g, 3] = full_mid[256:384, 256:384]
    ident = np.eye(128, dtype=np.float32)
    bf = ml_dtypes.bfloat16
    return {
        "k_cs": cs,
        "k_masks": masks.astype(bf),
        "k_pm": pm.astype(bf),
        "k_ident": ident,
        "k_identb": ident.astype(bf),
    }


def build(stop_after=None):
    nc = bass.Bass("TRN2", target_bir_lowering=False)

    def din(name, shape, dt=F32):
        return nc.dram_tensor(name, list(shape), dt, kind="ExternalInput")

    def dscr(name, shape, dt=BF):
        return nc.dram_tensor(name, list(shape), dt, kind="Internal")

    x_d = din("x", [SPC, SEQ, D])
    ctx_d = din("ctx", [SPC, CTX, D])
    cv_d = din("cvec", [128, NCH, 3])
    wmod_d = din("w_mod", [2, D, 9 * D])
    bmod_d = din("b_mod3", [2, 128, 72, 3])
    lng_d = din("ln_g", [128, 2, 3, NCH])
    lnb_d = din("ln_b", [128, 2, 3, NCH])
    wg_d = din("ffn_w_gate", [2, 2, D, DFF])
    wu_d = din("ffn_w_up", [2, 2, D, DFF])
    wd_d = din("ffn_w_down", [2, 2, DFF, D])
    win_d = din("mix_ab_w_in", [1, D, 1280])
    sink_d = din("attn_sink", [1, 8])
    pw_d = din("pool_w", [1, 4, 128, 128])
    psc_d = din("pool_scale", [128, 4])
    wout_d = din("mix_ab_w_out", [1, D, D])
    lin_d = din("lru_w_in", [1, D, 2 * D])
    cw_d = din("lru_conv_w", [128, NCH, 4])
    cb_d = din("lru_conv_b", [128, NCH])
    wa_d = din("lru_wa", [1, 2, 8, 128, 128])
    ba_d = din("lru_ba", [128, 2, NCH])
    wx_d = din("lru_wx", [1, 2, 8, 128, 128])
    bx_d = din("lru_bx", [128, 2, NCH])
    lam_d = din("lru_lambda", [128, 2, NCH])
    lout_d = din("lru_w_out", [1, D, D])
    kcs_d = din("k_cs", [128, 2, SEQ])
    kmask_d = din("k_masks", [128, 2, 512], BF)
    kpm_d = din("k_pm", [128, 4, 5, 128], BF)
    kid_d = din("k_ident", [128, 128])
    kidb_d = din("k_identb", [128, 128], BF)
    out_d = nc.dram_tensor("out", [SPC, SEQ, D], F32, kind="ExternalOutput")

    s_g = [[dscr("s_g%d%d" % (l, f), [D, DFF]) for f in range(2)] for l in range(2)]
    s_u = [[dscr("s_u%d%d" % (l, f), [D, DFF]) for f in range(2)] for l in range(2)]
    s_d = [[dscr("s_d%d%d" % (l, f), [DFF, D]) for f in range(2)] for l in range(2)]
    s_qk = dscr("s_qk", [D, 1280])
    s_qks = dscr("s_qks", [D, 640])
    s_pw = dscr("s_pw", [4, 128, 128])
    s_wo = dscr("s_wo", [D, D])
    s_lin = dscr("s_lin", [D, 2 * D])
    s_wa = dscr("s_wa", [2, 8, 128, 128])
    s_wx = dscr("s_wx", [2, 8, 128, 128])
    s_lo = dscr("s_lo", [D, D])

    st = contextlib.ExitStack()
    arena = st.enter_context(nc.sbuf_tensor("arena", [128, ARENA_BYTES // 2], BF))
    pb = [st.enter_context(nc.psum_tensor("pb%d" % i, [128, 512], F32)) for i in range(8)]
    S = Sched(nc)

    def V(off, shape, dt):
        n = int(np.prod(shape))
        esz = 4 if dt == F32 else 2
        assert off % 256 == 0 or True
        assert off + n * esz <= ARENA_BYTES, (off, shape)
        v = arena[:, off // 2:(off + n * esz) // 2]
        if dt == F32:
            v = v.bitcast(F32)
        if len(shape) == 2:
            v = v.rearrange("p (a b) -> p a b", a=shape[0])
        elif len(shape) == 3:
            v = v.rearrange("p (a b c) -> p a b c", a=shape[0], b=shape[1])
        return v

    o = 0
    h = V(o, [NCH, L], F32); o += NCH * L * 4
    ident = V(o, [128], F32); o += 512
    identb = V(o, [128], BF); o += 256
    onesN = V(o, [128], BF); o += 256
    onesA = V(o, [128], BF); o += 256
    onesB = V(o, [128], BF); o += 256
    masks = V(o, [2, 512], BF); o += 2048
    modv = [V(o + l * 1024, [72, 3], F32) for l in range(2)]; o += 2048
    lng = V(o, [2, 3, NCH], F32); o += 256
    lnb = V(o, [2, 3, NCH], F32); o += 256
    cvals = V(o, [4], F32); o += 256
    cvt = V(o, [NCH, 3], F32); o += 256
    expsink = V(o, [8], F32); o += 256
    psc = V(o, [4], F32); o += 256
    RB = o
    assert RB % 256 == 0

    def RV(off, shape, dt):
        return V(RB + off, shape, dt)

    r_bf = RV(0, [NCH, 512], BF)
    rsq_bf = RV(8192, [NCH, 512], BF)
    mean_sb = RV(16384, [512], F32)
    var_sb = RV(18432, [512], F32)
    rstd_sb = RV(20480, [512], F32)
    tn = [RV(22528 + 2048 * i, [512], F32) for i in range(2)]
    sg = [RV(26624 + 2048 * i, [512], F32) for i in range(2)]
    xmod = [RV(30720 + 8192 * i, [NCH, 512], BF) for i in range(2)]
    PH = 47104

    cnt = {"dma": 0}

    def MM(out, lhsT, rhs, start, stop):
        S.op("pe", lambda e: e.matmul(out, lhsT=lhsT, rhs=rhs, start=start, stop=stop), reads=[lhsT, rhs], writes=[out])

    def TR(out, in_, idn):
        S.op("pe", lambda e: e.transpose(out, in_, idn), reads=[in_, idn], writes=[out])

    def ACTV(out, in_, func, scale=1.0, bias=None, eng="act"):
        rd = [in_]
        kw = {}
        if not isinstance(scale, (int, float)):
            rd.append(scale)
        if bias is not None:
            if not isinstance(bias, (int, float)):
                rd.append(bias)
            kw["bias"] = bias
        S.op("act", lambda e: e.activation(out=out, in_=in_, func=func, scale=scale, **kw), reads=rd, writes=[out])

    def TT(eng, out, in0, in1, op):
        S.op(eng, lambda e: e.tensor_tensor(out=out, in0=in0, in1=in1, op=op), reads=[in0, in1], writes=[out])

    def TS(eng, out, in0, s1, op0, s2=None, op1=None):
        rd = [in0] + [s for s in (s1, s2) if s is not None and not isinstance(s, (int, float))]
        if op1 is None:
            S.op(eng, lambda e: e.tensor_scalar(out=out, in0=in0, scalar1=s1, scalar2=None, op0=op0), reads=rd, writes=[out])
        else:
            S.op(eng, lambda e: e.tensor_scalar(out=out, in0=in0, scalar1=s1, scalar2=s2, op0=op0, op1=op1), reads=rd, writes=[out])

    def STT(eng, out, in0, scalar, in1, op0, op1):
        rd = [in0, in1] + ([] if isinstance(scalar, (int, float)) else [scalar])
        S.op(eng, lambda e: e.scalar_tensor_tensor(out=out, in0=in0, scalar=scalar, in1=in1, op0=op0, op1=op1), reads=rd, writes=[out])

    def CP(eng, out, in_):
        if eng == "act":
            S.op("act", lambda e: e.activation(out=out, in_=in_, func=AF.Copy), reads=[in_], writes=[out])
        else:
            S.op(eng, lambda e: e.tensor_copy(out=out, in_=in_), reads=[in_], writes=[out])

    def MEMSET(eng, out, val):
        S.op(eng, lambda e: e.memset(out, val), writes=[out])

    def DMA(eng, out, in_, reads=(), writes=(), key=None, **kw):
        if key is None:
            cnt["dma"] += 1
            key = ("u", cnt["dma"])
        return S.op(eng, lambda e: e.dma_start(out=out, in_=in_, **kw), reads=list(reads), writes=list(writes), sem_key=key)

    def dap(t, offset, pat):
        return bass.AP(t, offset, [list(p) for p in pat])

    gate = {"k": None, "n": 0}

    def new_gate():
        gate["n"] += 1
        gate["k"] = ("gate", gate["n"])
        S.mark(gate["k"])

    def grd():
        return [gate["k"]] if gate["k"] is not None else []

    def cast2d(dst_t, src_t, src_off, nelem, rowlen, key):
        nrow = nelem // rowlen
        DMA("pool", dap(dst_t, 0, [[rowlen, nrow], [1, rowlen]]), dap(src_t, src_off, [[rowlen, nrow], [1, rowlen]]), reads=grd(), writes=[key], key=("c", key))

    castq = []

    def cast_ffn(l, f, tag=None):
        off = (l * 2 + f) * D * DFF
        first = (l, f) == (0, 0)
        pcs = []
        for g in range(11):
            for (dst_t, src_t, nm) in ((s_g[l][f], wg_d, "s_g"), (s_u[l][f], wu_d, "s_u")):
                wk = [("s_g", l, f, g), ("s_u", l, f, g)] if first else [(nm, l, f)]
                sk = ("c", "gu", l, f, g) if first else ("c", nm, l, f)
                pcs.append(lambda dst_t=dst_t, src_t=src_t, wk=wk, sk=sk, g=g: DMA(
                    "pool", dap(dst_t, g * 256, [[DFF, D], [1, 256]]), dap(src_t, off + g * 256, [[DFF, D], [1, 256]]),
                    reads=grd(), writes=wk, key=sk))
        for dp in range(4):
            wk = ("s_d", l, f, dp) if first else ("s_d", l, f)
            sk = ("c", "s_d", l, f, dp) if first else ("c", "s_d", l, f)
            pcs.append(lambda wk=wk, sk=sk, dp=dp: DMA(
                "pool", dap(s_d[l][f], dp * 256, [[D, DFF], [1, 256]]), dap(wd_d, off + dp * 256, [[D, DFF], [1, 256]]),
                reads=grd(), writes=[wk], key=sk))
        if tag is None:
            for p_ in pcs:
                p_()
        else:
            castq.extend((tag, p_) for p_ in pcs)

    def tick(n=1):
        for _ in range(n):
            if castq:
                new_gate()
                castq.pop(0)[1]()

    def flush_tag(tag):
        while any(t == tag for t, _ in castq):
            new_gate()
            castq.pop(0)[1]()

    def cast_ab():
        for kvh in range(2):
            DMA("pool", dap(s_qk, kvh * 64, [[1280, D], [128, 4], [1, 64]]),
                dap(win_d, kvh * 256, [[1280, D], [64, 4], [1, 64]]), reads=grd(), writes=[("s_qk", "q", kvh)], key=("c", "qk", kvh))
        DMA("pool", dap(s_qk, 512, [[1280, D], [1, 768]]), dap(win_d, 512, [[1280, D], [1, 768]]), reads=grd(), writes=[("s_qk", "r")], key=("c", "qk", 2))
        for kvh in range(2):
            for half in range(2):
                DMA("pool", dap(s_qks, kvh * 64 + half * 32, [[640, D], [128, 4], [1, 32]]),
                    dap(win_d, kvh * 256 + (1 - half) * 32, [[1280, D], [64, 4], [1, 32]]),
                    reads=grd(), writes=[("s_qks", kvh, half)], key=("c", "qks", kvh, half))
        for half in range(2):
            DMA("pool", dap(s_qks, 512 + half * 32, [[640, D], [64, 2], [1, 32]]),
                dap(win_d, 512 + (1 - half) * 32, [[1280, D], [64, 2], [1, 32]]),
                reads=grd(), writes=[("s_qks", "k", half)], key=("c", "qks", "k", half))
        cast2d(s_pw, pw_d, 0, 4 * 128 * 128, 2048, "s_pw")
        cast2d(s_wo, wout_d, 0, D * D, 1024, "s_wo")

    def cast_lru():
        cast2d(s_lin, lin_d, 0, D * 2 * D, 2048, "s_lin")
        cast2d(s_wa, wa_d, 0, 2 * 8 * 128 * 128, 2048, "s_wa")
        cast2d(s_wx, wx_d, 0, 2 * 8 * 128 * 128, 2048, "s_wx")
        cast2d(s_lo, lout_d, 0, D * D, 1024, "s_lo")

    AB_KEYS = [("s_qk", "q", 0), ("s_qk", "q", 1), ("s_qk", "r")] + [("s_qks", a, b) for a in (0, 1, "k") for b in (0, 1)]

    DMA("sp", ident, kid_d.ap(), writes=[ident])
    DMA("sp", identb, kidb_d.ap(), writes=[identb])
    DMA("sp", masks, kmask_d.ap(), writes=[masks])
    DMA("sp", lng, lng_d.ap(), writes=[lng])
    DMA("sp", lnb, lnb_d.ap(), writes=[lnb])
    DMA("sp", cvt, cv_d.ap(), writes=[cvt])
    DMA("sp", expsink, sink_d.ap()[0].partition_broadcast(128), writes=[expsink])
    MEMSET("dve", onesN, 1.0 / 1024.0)
    MEMSET("dve", onesA[:, 0:64], 1.0)
    MEMSET("dve", onesA[:, 64:128], 0.0)
    MEMSET("dve", onesB[:, 0:64], 0.0)
    MEMSET("dve", onesB[:, 64:128], 1.0)
    MEMSET("dve", cvals[:, 0:1], EPS_P)
    MEMSET("dve", cvals[:, 1:2], 1.0)
    MEMSET("dve", cvals[:, 2:3], 0.0)
    cast_ffn(0, 0)

    ACTV(cvt, cvt, AF.Silu)
    bm = RV(PH + 2 * 8192, [72, 3], F32)
    def mod_finalize(l, mps, bm_):
        TT("dve", modv[l], mps, bm_, ALU.add)
        for i in range(3):
            sc = modv[l][:, (3 * i + 1) * 8:(3 * i + 2) * 8, :]
            TS("dve", sc, sc, 1.0, ALU.add)
            gt = modv[l][:, (3 * i + 2) * 8:(3 * i + 3) * 8, :]
            TS("dve", gt, gt, (1.0 if i == 1 else 0.5) / ALPHA, ALU.mult)

    def mod_block(l, b, slot, msb, bank, skey):
        DMA("sp", slot, wmod_d.ap()[l].rearrange("(kc p) n -> p kc n", p=128)[:, :, b * 256:(b + 1) * 256], writes=[slot], key=skey)
        for kc in range(NCH):
            MM(bank[0:3, 256:512], cvt[:, kc, :], slot[:, kc, :], kc == 0, kc == NCH - 1)
        CP("dve", msb[0:3, 0:256], bank[0:3, 256:512])
        for q in range(2):
            fc = 2 * b + q
            TR(bank[:, fc * 3:fc * 3 + 3], msb[0:3, q * 128:(q + 1) * 128], ident[0:3, 0:3])

    wm0 = [RV(PH + i * 8192, [NCH, 256], F32) for i in range(2)]
    DMA("sp", bm, bmod_d.ap()[0], writes=[bm])
    for b in range(36):
        mod_block(0, b, wm0[b % 2], sg[0], pb[3], ("wm", b % 2))
    mod_finalize(0, pb[3][:, 0:216].rearrange("p (a b) -> p a b", b=3), bm)

    def mod1_step(k, SBo):
        slot = RV(SBo + 10240, [NCH, 256], F32)
        bm1 = sg[1][:, 0:216].rearrange("p (a b) -> p a b", b=3)
        if k == 0:
            DMA("sp", bm1, bmod_d.ap()[1], writes=[bm1])
        for b in (2 * k, 2 * k + 1):
            mod_block(1, b, slot, sg[0], pb[3], ("wm1", 0))
        if k == 17:
            mod_finalize(1, pb[3][:, 0:216].rearrange("p (a b) -> p a b", b=3), bm1)

    def mvec(l, v, ch, col):
        return modv[l][:, v * 8 + ch, col:col + 1]

    ACTV(expsink, expsink, AF.Exp)

    def modulate(dst, l, i, c0, T, col, dcol0=0):
        for ch in range(NCH):
            ACTV(dst[:, ch, dcol0:dcol0 + T], h[:, ch, c0:c0 + T], AF.Identity, scale=mvec(l, 3 * i + 1, ch, col), bias=mvec(l, 3 * i, ch, col))

    def resid(ch, yps, l, i, c0, T, col):
        hs = h[:, ch, c0:c0 + T]
        STT("dve", hs, yps, mvec(l, 3 * i + 2, ch, col), hs, ALU.mult, ALU.add)
        CP("act", r_bf[:, ch, :T], hs)
        ACTV(rsq_bf[:, ch, :T], hs, AF.Square)

    def layernorm(l, i, c0, T):
        for ch in range(NCH):
            MM(pb[6][:, :T], onesN, r_bf[:, ch, :T], ch == 0, ch == NCH - 1)
        for ch in range(NCH):
            MM(pb[7][:, :T], onesN, rsq_bf[:, ch, :T], ch == 0, ch == NCH - 1)
        CP("act", mean_sb[:, :T], pb[6][:, :T])
        TT("dve", var_sb[:, :T], mean_sb[:, :T], mean_sb[:, :T], ALU.mult)
        TT("dve", var_sb[:, :T], pb[7][:, :T], var_sb[:, :T], ALU.subtract)
        ACTV(rstd_sb[:, :T], var_sb[:, :T], AF.Sqrt, bias=cvals[:, 0:1])
        S.op("dve", lambda e: e.reciprocal(out=rstd_sb[:, :T], in_=rstd_sb[:, :T]), reads=[rstd_sb[:, :T]], writes=[rstd_sb[:, :T]])
        def mk(ch):
            def _f():
                t_ = tn[ch % 2]
                hs = h[:, ch, c0:c0 + T]
                TT("dve", t_[:, :T], hs, mean_sb[:, :T], ALU.subtract)
                TT("dve", t_[:, :T], t_[:, :T], rstd_sb[:, :T], ALU.mult)
                ACTV(hs, t_[:, :T], AF.Identity, scale=lng[:, l, i, ch:ch + 1], bias=lnb[:, l, i, ch:ch + 1])
            return _f
        for ch in range(NCH):
            pending.append(mk(ch))

    pending = []

    def drain(n=None):
        k = len(pending) if n is None else min(n, len(pending))
        for _ in range(k):
            pending.pop(0)()

    hid = RV(PH, [NJ, 512], BF)
    wA = [[RV(PH + 22528 + s * 8192 + t * 4096, [NCH, 256], BF) for t in range(2)] for s in range(3)]
    wB = [RV(PH + 22528 + 24576 + s * 11264, [NJ, 256], BF) for s in range(2)]

    def ffn(l, f, seq, tiles, hooks=None):
        i = 0 if f == 0 else 2
        gsrc = s_g[l][f].ap().rearrange("(kc p) n -> p kc n", p=128)
        usrc = s_u[l][f].ap().rearrange("(kc p) n -> p kc n", p=128)
        dsrc = s_d[l][f].ap().rearrange("(j p) n -> p j n", p=128)
        steps = []
        for ti in range(len(tiles)):
            for g in range(11):
                steps.append(("A", ti, g))
            for dp in range(4):
                steps.append(("B", ti, dp))
        ia = {"A": 0, "B": 0}
        jcount = 0
        for k in range(len(steps)):
            kind, ti, g = steps[k]
            if seq == 0 and k % 3 == 2:
                tick()
            if kind == "A":
                s = ia["A"] % 3
                ia["A"] += 1
                DMA("sp", wA[s][0], gsrc[:, :, g * 256:(g + 1) * 256], reads=[("s_g", l, f, g) if (l, f) == (0, 0) else ("s_g", l, f)], writes=[wA[s][0]], key=("wA", s, 0))
                DMA("sp", wA[s][1], usrc[:, :, g * 256:(g + 1) * 256], reads=[("s_u", l, f, g) if (l, f) == (0, 0) else ("s_u", l, f)], writes=[wA[s][1]], key=("wA", s, 1))
            else:
                s = ia["B"] % 2
                ia["B"] += 1
                DMA("sp", wB[s], dsrc[:, :, g * 256:(g + 1) * 256], reads=[("s_d", l, f, g) if (l, f) == (0, 0) else ("s_d", l, f)], writes=[wB[s]], key=("wB", s))
            kind, ti, g = steps[k]
            c0, T, col = tiles[ti]
            if col is None:
                col = seq
            xm = xmod[ti % 2]
            if kind == "A":
                if g == 0 and ti == 0:
                    modulate(xm, l, i, c0, T, col)
                for jj in range(2):
                    j = 2 * g + jj
                    ga = pb[jcount % 2]
                    ub = pb[2 + jcount % 2]
                    sgt = sg[jcount % 2]
                    jcount += 1
                    for kc in range(NCH):
                        MM(ga[:, :T], wA[s][0][:, kc, jj * 128:(jj + 1) * 128], xm[:, kc, :T], kc == 0, kc == NCH - 1)
                    for kc in range(NCH):
                        MM(ub[:, :T], wA[s][1][:, kc, jj * 128:(jj + 1) * 128], xm[:, kc, :T], kc == 0, kc == NCH - 1)
                    ACTV(sgt[:, :T], ga[:, :T], AF.Silu)
                    TT("dve", hid[:, j, :T], sgt[:, :T], ub[:, :T], ALU.mult)
                    drain(1)
            else:
                if g == 0 and ti + 1 < len(tiles):
                    c0n, Tn, coln = tiles[ti + 1]
                    drain()
                    modulate(xmod[(ti + 1) % 2], l, i, c0n, Tn, seq if coln is None else coln)
                for dd in range(2):
                    ch = 2 * g + dd
                    yps = pb[4 + dd]
                    for j in range(NJ):
                        MM(yps[:, :T], wB[s][:, j, dd * 128:(dd + 1) * 128], hid[:, j, :T], j == 0, j == NJ - 1)
                    resid(ch, yps[:, :T], l, i, c0, T, col)
                if g == 3:
                    drain()
                    layernorm(l, 0 if f == 0 else 2, c0, T)
                    if hooks and ti in hooks:
                        hooks[ti]()

    stg = [RV(PH + i * 4096, [D], F32) for i in range(2)]

    def load_seq(seq):
        for blk in range(L // 128):
            sl = stg[blk % 2]
            if blk < 2:
                src = ctx_d.ap()[seq, blk * 128:(blk + 1) * 128, :]
            else:
                src = x_d.ap()[seq, (blk - 2) * 128:(blk - 1) * 128, :]
            DMA("sp", sl, src, writes=[sl], key=("stg", blk % 2))
            for half in range(2):
                bank = pb[(blk * 2 + half) % 4]
                for q in range(4):
                    ch = half * 4 + q
                    TR(bank[:, q * 128:(q + 1) * 128], sl[:, ch * 128:(ch + 1) * 128], ident)
                dst = h[:, half * 4:half * 4 + 4, blk * 128:(blk + 1) * 128]
                srcp = bank[:].rearrange("p (a b) -> p a b", b=128)
                CP("act" if half == 0 else "dve", dst, srcp)

    out_dmas = []

    def store_seq(seq):
        for blk in range(SEQ // 128):
            sl = stg[blk % 2]
            c0 = CTX + blk * 128
            for half in range(2):
                bank = pb[(blk * 2 + half) % 4]
                for q in range(4):
                    ch = half * 4 + q
                    TR(bank[:, q * 128:(q + 1) * 128], h[:, ch, c0:c0 + 128], ident)
                CP("act" if half == 0 else "dve", sl[:, half * 512:(half + 1) * 512], bank[:])
            out_dmas.append(DMA("sp", out_d.ap()[seq, blk * 128:(blk + 1) * 128, :], sl, reads=[sl], key=("ost", blk % 2)))

    q_rot = RV(PH, [4, L], BF)
    kTk = [RV(PH + 18432 + i * 4608, [L], BF) for i in range(2)]
    vtk = [RV(PH + 27648 + i * 4608, [L // 128, 128], BF) for i in range(2)]
    poolT = RV(PH + 36864, [4, L], BF)
    SB = PH + 55296
    w_qkv = RV(SB, [NCH, 1408], BF)
    csb = [RV(i * 4096, [2, 512], F32) for i in range(2)]
    rtmp = [RV(8192 + i * 2048, [512], F32) for i in range(2)]
    w_u = RV(SB, [NCH, 512], BF)
    utok = RV(SB + 8192, [8, 512], BF)
    w_pool = RV(SB + 16384, [4, 128], BF)
    pmt = RV(SB + 17408, [4, 5, 128], BF)
    dTt = [RV(SB + 22528 + i * 1024, [4, 128], BF) for i in range(2)]
    pT = [RV(SB + i * 1024, [512], BF) for i in range(4)]
    den = [RV(SB + 4096 + i * 2048, [512], F32) for i in range(2)]
    sinkrow = RV(SB + 8192, [512], F32)
    w_o = RV(SB, [8, D], BF)

    def mixer_ab(seq):
        l = 0
        src = s_qk.ap().rearrange("(kc p) n -> p kc n", p=128)
        srcs = s_qks.ap().rearrange("(kc p) n -> p kc n", p=128)
        DMA("sp", w_qkv[:, :, 0:768], src[:, :, 0:768], reads=AB_KEYS, writes=[w_qkv[:, :, 0:768]])
        DMA("sp", w_qkv[:, :, 768:1408], srcs, reads=AB_KEYS, writes=[w_qkv[:, :, 768:1408]])
        MEMSET("dve", kTk[0][64:128, :], 0.0)
        MEMSET("dve", kTk[1][0:64, :], 0.0)
        MEMSET("dve", vtk[0][:, :, 64:128], 0.0)
        MEMSET("dve", vtk[1][:, :, 0:64], 0.0)
        for ti, (c0, T, col) in enumerate(TILES):
            if col is None:
                col = seq
            xm = xmod[ti % 2]
            lat = ti > 0
            if lat:
                cs_ = csb[ti % 2]
                DMA("sp", cs_, kcs_d.ap()[:, :, (ti - 1) * 512:ti * 512], writes=[cs_], key=("cs", ti % 2))
                drain()
            modulate(xm, l, 1, c0, T, col)
            for c in range(5):
                drain(2)
                pq = pb[(c % 2) * 2]
                pqs = pb[(c % 2) * 2 + 1]
                wc = c * 128
                dst = q_rot[:, c, c0:c0 + T] if c < 4 else None
                for kc in range(NCH):
                    MM(pq[:, :T], w_qkv[:, kc, wc:wc + 128], xm[:, kc, :T], kc == 0, kc == NCH - 1)
                if lat:
                    for kc in range(NCH):
                        MM(pqs[:, :T], w_qkv[:, kc, 768 + wc:768 + wc + 128], xm[:, kc, :T], kc == 0, kc == NCH - 1)
                    TT("dve", rtmp[0][:, :T], pq[:, :T], cs_[:, 0, :T], ALU.mult)
                    TT("dve", rtmp[1][:, :T], pqs[:, :T], cs_[:, 1, :T], ALU.mult)
                    if c < 4:
                        TT("dve", dst, rtmp[0][:, :T], rtmp[1][:, :T], ALU.add)
                    else:
                        for hv in range(2):
                            pr_ = slice(hv * 64, hv * 64 + 64)
                            TT("dve", kTk[hv][pr_, c0:c0 + T], rtmp[0][pr_, :T], rtmp[1][pr_, :T], ALU.add)
                else:
                    if c < 4:
                        CP("act", dst, pq[:, :T])
                    else:
                        for hv in range(2):
                            pr_ = slice(hv * 64, hv * 64 + 64)
                            CP("act", kTk[hv][pr_, c0:c0 + T], pq[pr_, :T])
            nblk = T // 128
            for b in range(nblk):
                for kc in range(NCH):
                    MM(pb[4][:, b * 128:(b + 1) * 128], xm[:, kc, b * 128:(b + 1) * 128], w_qkv[:, kc, 640:768], kc == 0, kc == NCH - 1)
            vps = pb[4][:, :T].rearrange("p (a b) -> p a b", b=128)
            CP("act", vtk[0][:, c0 // 128:c0 // 128 + nblk, 0:64], vps[:, :, 0:64])
            CP("dve", vtk[1][:, c0 // 128:c0 // 128 + nblk, 64:128], vps[:, :, 64:128])
        if _ABP < 2:
            return
        DMA("sp", w_u, src[:, :, 768:1280], reads=AB_KEYS, writes=[w_u])
        DMA("sp", w_pool, s_pw.ap().rearrange("g c e -> c g e"), reads=["s_pw"], writes=[w_pool])
        DMA("sp", pmt, kpm_d.ap(), writes=[pmt])
        DMA("sp", psc, psc_d.ap(), writes=[psc])

        def ring(cb):
            return (cb + 6) % 8 if cb < 2 else (cb - 2) % 8

        def pool_block(cb):
            if _ABQ < 2:
                return
            first = cb in (0, 2)
            last = cb in (1, 17)
            bank = pb[2 + cb % 2]
            for g in range(4):
                terms = []
                if not first:
                    terms.append((cb - 1, 0))
                terms.append((cb, 1 if first else (3 if last else 2)))
                if not last:
                    terms.append((cb + 1, 4))
                for n_, (sb_, var) in enumerate(terms):
                    MM(bank[:, g * 128:(g + 1) * 128], utok[:, ring(sb_), g * 128:(g + 1) * 128], pmt[:, g, var, :], n_ == 0, n_ == len(terms) - 1)
            dt_ = dTt[cb % 2]
            CP("dve", dt_, bank[:].rearrange("p (a b) -> p a b", b=128))
            if _ABQ < 3:
                return
            bank2 = pb[4 + cb % 2]
            for g in range(4):
                MM(bank2[:, g * 128:(g + 1) * 128], w_pool[:, g, :], dt_[:, g, :], True, True)
            for g in range(4):
                ACTV(poolT[:, g, cb * 128:(cb + 1) * 128], bank2[:, g * 128:(g + 1) * 128], AF.Identity, scale=psc[:, g:g + 1])

        for ti, (c0, T, col) in enumerate(TILES):
            if col is None:
                col = seq
            xm = xmod[ti % 2]
            if _ABQ < 1:
                continue
            modulate(xm, l, 1, c0, T, col)
            nblk = T // 128
            for b in range(nblk):
                cb = c0 // 128 + b
                bank = pb[cb % 2]
                for kc in range(NCH):
                    MM(bank[:], xm[:, kc, b * 128:(b + 1) * 128], w_u[:, kc, :], kc == 0, kc == NCH - 1)
                CP("act", utok[:, ring(cb), :], bank[:])
            if ti == 0:
                pool_block(0)
                pool_block(1)
            else:
                b0 = c0 // 128
                if ti > 1:
                    pool_block(b0 - 1)
                for b in range(3):
                    pool_block(b0 + b)
                if ti == 4:
                    pool_block(b0 + 3)
        if _ABP < 3:
            return
        for kvh in range(2):
            for g in range(4):
                hd = kvh * 4 + g
                sl = sinkrow[kvh * 64:(kvh + 1) * 64, g * 128:(g + 1) * 128]
                ACTV(sl, ident[kvh * 64:(kvh + 1) * 64, :], AF.Identity, scale=0.0, bias=expsink[kvh * 64:(kvh + 1) * 64, hd:hd + 1])
        items = []
        for qb in range(L // 128):
            if qb < 2:
                keys = [(0, None), (1, None)]
            else:
                keys = []
                if qb - 1 >= 2:
                    keys.append((qb - 1, 0))
                keys.append((qb, None))
                if qb + 1 <= 17:
                    keys.append((qb + 1, 1))
                keys += [(0, None), (1, None)]
            for kvh in range(2):
                for n_, (kb, mk) in enumerate(keys):
                    items.append((qb, kvh, kb, mk, kvh == 0 and n_ == 0, kvh == 1 and n_ == len(keys) - 1))
        NI = len(items)
        LA = 2
        for t in range(NI + LA):
            if t < NI:
                qb, kvh, kb, mk, fst, lst = items[t]
                qc = qb * 128
                sps = pb[t % 3]
                ptt = pT[t % 4]
                MM(sps[:], kTk[kvh][:, kb * 128:(kb + 1) * 128], q_rot[:, :, qc:qc + 128], True, mk is None)
                if mk is not None:
                    MM(sps[:], identb, masks[:, mk, :], False, True)
                ACTV(ptt, sps[:], AF.Exp, scale=0.125)
            u_ = t - LA
            if u_ >= 0:
                qb, kvh, kb, mk, fst, lst = items[u_]
                qc = qb * 128
                ptt = pT[u_ % 4]
                ops_ = pb[4 + qb % 2]
                dps_ = pb[6 + qb % 2]
                MM(ops_[:], vtk[kvh][:, kb, :], ptt, fst, lst)
                MM(dps_[:], (onesA if kvh == 0 else onesB), ptt, fst, lst)
                if lst:
                    if seq == 0:
                        mod1_step(qb, SB)
                        tick()
                    dn = den[qb % 2]
                    TT("dve", dn, dps_[:], sinkrow, ALU.add)
                    S.op("dve", lambda e, dn=dn: e.reciprocal(out=dn, in_=dn), reads=[dn], writes=[dn])
                    TT("dve", q_rot[:, :, qc:qc + 128], ops_[:].rearrange("p (a b) -> p a b", b=128), dn.rearrange("p (a b) -> p a b", b=128), ALU.mult)
        if _ABP < 4:
            return
        wo_s = s_wo.ap()
        for two in range(2):
            DMA("sp", w_o[two * 64:(two + 1) * 64, 0:4, :], dap(s_wo, two * 256 * D, [[D, 64], [64 * D, 4], [1, D]]), reads=["s_wo"], writes=[w_o[:, 0:4, :]])
        DMA("sp", w_o[:, 4:8, :], wo_s[512:1024, :].rearrange("(g p) n -> p g n", p=128), reads=["s_wo"], writes=[w_o[:, 4:8, :]])
        for ti, (c0, T, col) in enumerate(TILES):
            if col is None:
                col = seq
            for ch in range(NCH):
                yps = pb[ch % 2]
                for g in range(4):
                    MM(yps[:, :T], w_o[:, g, ch * 128:(ch + 1) * 128], q_rot[:, g, c0:c0 + T], g == 0, False)
                for g in range(4):
                    MM(yps[:, :T], w_o[:, 4 + g, ch * 128:(ch + 1) * 128], poolT[:, g, c0:c0 + T], False, g == 3)
                resid(ch, yps[:, :T], l, 1, c0, T, col)
                drain(1)
            drain()
            layernorm(l, 1, c0, T)

    LA = RV(0, [L], F32)
    LBb = RV(9216, [L], F32)
    wax = [RV(23040 + i * 1024, [4, 128], BF) for i in range(2)]
    lpar = RV(43776, [5, 2, NCH], F32)
    cwt = RV(43776 + 512, [NCH, 4], F32)
    cbt = RV(43776 + 768, [NCH], F32)
    xfull = RV(PH, [NCH, L], BF)
    gy = RV(PH + 36864, [4, SEQ], BF)
    UB = PH + 36864 + 16384
    UU = [(RV(25088, [2368], F32), RV(34560, [L], F32), RV(18432, [L], BF)),
          (RV(UB, [2368], F32), RV(UB + 9472, [L], F32), RV(UB + 18688, [L], BF))]
    w_loh = RV(UB + 9472, [4, D], BF)
    wsl = [RV(UB + 23296 + i * 4096, [NCH, 2, 128], BF) for i in range(2)]
    CO = 1
    LO = 260

    def mixer_lru(seq):
        l = 1
        DMA("sp", lpar[:, 0], ba_d.ap(), writes=[lpar[:, 0]])
        DMA("sp", lpar[:, 1], bx_d.ap(), writes=[lpar[:, 1]])
        DMA("sp", lpar[:, 2], lam_d.ap(), writes=[lpar[:, 2]])
        DMA("sp", cwt, cw_d.ap(), writes=[cwt])
        DMA("sp", cbt, cb_d.ap(), writes=[cbt])
        ACTV(lpar[:, 2], lpar[:, 2], AF.Exp, scale=-1.0)
        ACTV(lpar[:, 2], lpar[:, 2], AF.Ln, bias=cvals[:, 1:2])
        TS("dve", lpar[:, 2], lpar[:, 2], -8.0, ALU.mult)
        for ti, (c0, T, col) in enumerate(TILES):
            if col is None:
                col = seq
            if ti == 1:
                drain()
            modulate(xfull, l, 1, c0, T, col, dcol0=c0)
        lsrc = s_lin.ap().rearrange("(kc p) n -> p kc n", p=128)
        bk = {"n": 0}

        def nbank():
            bk["n"] += 1
            return pb[2 + bk["n"] % 6]

        def front_a(c):
            U1, U2, ubf = UU[c % 2]
            ws = wsl[c % 2]
            wx_ = wax[c % 2]
            DMA("sp", ws[:, :, 0, :], lsrc[:, :, c * 128:(c + 1) * 128], reads=["s_lin"], writes=[ws[:, :, 0, :]], key=("wsl", c % 2, 0))
            DMA("sp", ws[:, :, 1, :], lsrc[:, :, D + c * 128:D + (c + 1) * 128], reads=["s_lin"], writes=[ws[:, :, 1, :]], key=("wsl", c % 2, 1))
            DMA("sp", wx_[:, 0:4:2, :], s_wa.ap()[:, c].rearrange("d i j -> i d j"), reads=["s_wa"], writes=[wx_[:, 0:4:2, :]], key=("wax", c % 2, 0))
            DMA("sp", wx_[:, 1:4:2, :], s_wx.ap()[:, c].rearrange("d i j -> i d j"), reads=["s_wx"], writes=[wx_[:, 1:4:2, :]], key=("wax", c % 2, 1))
            MEMSET("dve", U1[:, 0:1], 0.0)
            MEMSET("dve", U1[:, 257:260], 0.0)
            MEMSET("dve", U1[:, 2308:2310], 0.0)
            for ti, (c0, T, col) in enumerate(TILES):
                bank = pb[ti % 2]
                for kc in range(NCH):
                    MM(bank[:, :T], ws[:, kc, 1, :], xfull[:, kc, c0:c0 + T], kc == 0, kc == NCH - 1)
                uo = CO + c0 if ti == 0 else LO + (c0 - CTX)
                CP("act", U1[:, uo:uo + T], bank[:, :T])
            for (uo, c0, T) in ((CO, 0, CTX), (LO, CTX, SEQ)):
                ACTV(U2[:, c0:c0 + T], U1[:, uo - 1:uo - 1 + T], AF.Identity, scale=cwt[:, c, 0:1], bias=cbt[:, c:c + 1])

        def front_b(c):
            U1, U2, ubf = UU[c % 2]
            for (uo, c0, T) in ((CO, 0, CTX), (LO, CTX, SEQ)):
                for tap in range(1, 4):
                    STT("dve", U2[:, c0:c0 + T], U1[:, uo - 1 + tap:uo - 1 + tap + T], cwt[:, c, tap:tap + 1], U2[:, c0:c0 + T], ALU.mult, ALU.add)

        def front_c(c):
            U1, U2, ubf = UU[c % 2]
            for (c0_, T_, _c) in TILES:
                CP("act", ubf[:, c0_:c0_ + T_], U2[:, c0_:c0_ + T_])

        def direction(c, dr):
            U1, U2, ubf = UU[c % 2]
            wx_ = wax[c % 2]
            T_ = (U1 if dr == 0 else U2)[:, 0:L]
            for ti, (c0, T, col) in enumerate(TILES):
                bank = nbank()
                MM(bank[:, :T], wx_[:, 2 * dr, :], ubf[:, c0:c0 + T], True, True)
                ACTV(LA[:, c0:c0 + T], bank[:, :T], AF.Sigmoid, bias=lpar[:, 0, dr, c:c + 1])
                bank2 = nbank()
                MM(bank2[:, :T], wx_[:, 2 * dr + 1, :], ubf[:, c0:c0 + T], True, True)
                ACTV(LBb[:, c0:c0 + T], bank2[:, :T], AF.Sigmoid, bias=lpar[:, 1, dr, c:c + 1])
            ACTV(LA, LA, AF.Exp, scale=lpar[:, 2, dr, c:c + 1])
            TT("dve", LBb, LBb, U2, ALU.mult)
            ACTV(T_, LA, AF.Square)
            ACTV(T_, T_, AF.Sqrt, scale=-1.0, bias=cvals[:, 1:2])
            TT("dve", LBb, LBb, T_, ALU.mult)
            so = T_
            if dr == 0:
                S.op("dve", lambda e: e.tensor_tensor_scan(out=so[:, 0:CTX], data0=LA[:, 0:CTX], data1=LBb[:, 0:CTX], initial=0.0, op0=ALU.mult, op1=ALU.add),
                     reads=[LA[:, 0:CTX], LBb[:, 0:CTX]], writes=[so[:, 0:CTX]])
                S.op("dve", lambda e: e.tensor_tensor_scan(out=so[:, CTX:L], data0=LA[:, CTX:L], data1=LBb[:, CTX:L], initial=so[:, CTX - 1:CTX], op0=ALU.mult, op1=ALU.add),
                     reads=[LA[:, CTX:L], LBb[:, CTX:L], so[:, CTX - 1:CTX]], writes=[so[:, CTX:L]])
            else:
                S.op("dve", lambda e: e.tensor_tensor_scan(out=so[:, CTX - 1::-1], data0=LA[:, CTX - 1::-1], data1=LBb[:, CTX - 1::-1], initial=0.0, op0=ALU.mult, op1=ALU.add),
                     reads=[LA[:, 0:CTX], LBb[:, 0:CTX]], writes=[so[:, 0:CTX]])
                S.op("dve", lambda e: e.tensor_tensor_scan(out=so[:, L - 1:CTX - 1:-1], data0=LA[:, L - 1:CTX - 1:-1], data1=LBb[:, L - 1:CTX - 1:-1], initial=so[:, 0:1], op0=ALU.mult, op1=ALU.add),
                     reads=[LA[:, CTX:L], LBb[:, CTX:L], so[:, 0:1]], writes=[so[:, CTX:L]])

        def tail(c):
            U1, U2, ubf = UU[c % 2]
            ws = wsl[c % 2]
            cg = c % 4
            TT("dve", LA[:, CTX:L], U1[:, CTX:L], U2[:, CTX:L], ALU.add)
            for ti in range(1, 5):
                c0, T, _ = TILES[ti]
                bank = pb[ti % 2]
                for kc in range(NCH):
                    MM(bank[:, :T], ws[:, kc, 0, :], xfull[:, kc, c0:c0 + T], kc == 0, kc == NCH - 1)
                ACTV(LBb[:, c0:c0 + T], bank[:, :T], AF.Gelu_apprx_tanh)
                TT("dve", gy[:, cg, c0 - CTX:c0 - CTX + T], LBb[:, c0:c0 + T], LA[:, c0:c0 + T], ALU.mult)

        def outproj(grp):
            DMA("sp", w_loh, s_lo.ap()[grp * 512:(grp + 1) * 512, :].rearrange("(c p) n -> p c n", p=128), reads=["s_lo"], writes=[w_loh])
            for ti in range(1, 5):
                c0, T, _ = TILES[ti]
                for ch in range(NCH):
                    yps = pb[ch % 2]
                    for c4 in range(4):
                        MM(yps[:, :T], w_loh[:, c4, ch * 128:(ch + 1) * 128], gy[:, c4, c0 - CTX:c0 - CTX + T], c4 == 0, c4 == 3)
                    if grp == 0:
                        hs = h[:, ch, c0:c0 + T]
                        STT("dve", hs, yps[:, :T], mvec(l, 5, ch, seq), hs, ALU.mult, ALU.add)
                    else:
                        resid(ch, yps[:, :T], l, 1, c0, T, seq)
                        drain(1)
                if grp == 1:
                    drain()
                    layernorm(l, 1, c0, T)

        front_a(0)
        front_b(0)
        front_c(0)
        for c in range(NCH):
            nxt = c + 1 < NCH
            if nxt:
                front_a(c + 1)
                front_b(c + 1)
            direction(c, 0)
            if nxt:
                front_c(c + 1)
            direction(c, 1)
            tail(c)
            if c % 4 == 3:
                outproj(c // 4)

    for seq in range(SPC):
        cnt["dma"] = 1000
        load_seq(seq)
        stage = 0

        def done():
            return stop_after is not None and stage >= stop_after
        if seq == 0:
            castq.append(("ab", cast_ab))
            cast_ffn(0, 1, "f01")
            cast_ffn(1, 0, "f10")
            castq.append(("lru", cast_lru))
            cast_ffn(1, 1, "f11")
        ffn(0, 0, seq, TILES); stage = 1
        if not done():
            flush_tag("ab")
            mixer_ab(seq); stage = 2
        if not done():
            flush_tag("f01")
            ffn(0, 1, seq, TILES); stage = 3
        if not done():
            flush_tag("f10")
            ffn(1, 0, seq, TILES); stage = 4
        if not done():
            flush_tag("lru")
            mixer_lru(seq); stage = 5
        if not done():
            flush_tag("f11")
            ffn(1, 1, seq, TILES[1:]); stage = 6
        drain()
        store_seq(seq)

    S.emit(final_dma_waits=out_dmas)
    st.close()
    return nc, S


_CACHE = {}


def _prep_inputs(inputs, ncores=NCORES):
    f = lambda a: np.ascontiguousarray(np.asarray(a, dtype=np.float32))
    x = f(inputs["x"]); c = f(inputs["c"]); ctx = f(inputs["ctx"]); c_ctx = f(inputs["c_ctx"])
    shared = {k: f(inputs[k]) for k in ("w_mod", "ffn_w_gate", "ffn_w_up", "ffn_w_down", "mix_ab_w_in", "attn_sink", "pool_w",
                                         "mix_ab_w_out", "lru_w_in", "lru_wa", "lru_wx", "lru_w_out")}
    b_mod = f(inputs["b_mod"])
    bm = b_mod.reshape(2, 72, 128).transpose(0, 2, 1)
    shared["b_mod3"] = np.ascontiguousarray(np.repeat(bm[:, :, :, None], 3, axis=3))
    pl = lambda a: np.ascontiguousarray(a)
    shared["ln_g"] = pl(f(inputs["ln_g"]).reshape(2, 3, NCH, 128).transpose(3, 0, 1, 2))
    shared["ln_b"] = pl(f(inputs["ln_b"]).reshape(2, 3, NCH, 128).transpose(3, 0, 1, 2))
    shared["pool_scale"] = pl(f(inputs["pool_scale"]).reshape(4, 128).T)
    shared["lru_conv_w"] = pl(f(inputs["lru_conv_w"]).reshape(4, NCH, 128).transpose(2, 1, 0))
    shared["lru_conv_b"] = pl(f(inputs["lru_conv_b"]).reshape(NCH, 128).T)
    for k in ("lru_ba", "lru_bx", "lru_lambda"):
        shared[k] = pl(f(inputs[k]).reshape(2, NCH, 128).transpose(2, 0, 1))
    shared.update(_host_consts())
    in_maps = []
    for i in range(ncores):
        m = dict(shared)
        m["x"] = np.ascontiguousarray(x[SPC * i:SPC * (i + 1)])
        m["ctx"] = np.ascontiguousarray(ctx[SPC * i:SPC * (i + 1)])
        cv = np.stack([c[SPC * i], c[SPC * i + 1], c_ctx], axis=-1)
        m["cvec"] = np.ascontiguousarray(cv.reshape(NCH, 128, 3).transpose(1, 0, 2))
        in_maps.append(m)
    return in_maps


def kernel(**inputs):
    if "nc" not in _CACHE:
        _CACHE["nc"] = build()[0]
    nc = _CACHE["nc"]
    in_maps = _prep_inputs(inputs)
    res = run_bass_kernel_spmd(nc, in_maps, core_ids=list(range(NCORES)))
    return np.concatenate([r["out"] for r in res.results], axis=0)
```

```python
import contextlib
import math
import numpy as np
import ml_dtypes
import concourse.bass as bass
import concourse.mybir as mybir
from concourse.bass_utils import run_bass_kernel_spmd

F32 = mybir.dt.float32
BF = mybir.dt.bfloat16
AF = mybir.ActivationFunctionType
ALU = mybir.AluOpType

ENGINES = ("pe", "act", "dve", "pool", "sp")
GRAN = 256
_DTSIZE = {}


def _dtsize(dt):
    s = _DTSIZE.get(dt)
    if s is None:
        name = str(dt)
        s = 4 if "32" in name else (2 if "16" in name else (8 if "64" in name else 1))
        _DTSIZE[dt] = s
    return s


def ap_granules(ap):
    t = ap.tensor
    name = t.name
    pat = ap.ap
    esz = _dtsize(ap.dtype)
    pstride = pat[0][0]
    off = ap.offset
    if pstride > 0:
        off = off % pstride
    dims = [(s, c) for (s, c) in pat[1:] if c > 1]
    if not dims:
        return name, {(off * esz) // GRAN}
    s_in, c_in = dims[-1]
    outer = dims[:-1]
    nouter = 1
    for _, c in outer:
        nouter *= c
    gr = set()
    if nouter > 256 or abs(s_in) > 1:
        lo = hi = off
        for s, c in dims:
            if s >= 0:
                hi += s * (c - 1)
            else:
                lo += s * (c - 1)
        gr.update(range((lo * esz) // GRAN, ((hi + 1) * esz - 1) // GRAN + 1))
        return name, gr
    starts = [off]
    for s, c in outer:
        starts = [b + s * i for b in starts for i in range(c)]
    for b in starts:
        if s_in >= 0:
            lo, hi = b, b + s_in * (c_in - 1)
        else:
            lo, hi = b + s_in * (c_in - 1), b
        gr.update(range((lo * esz) // GRAN, ((hi + 1) * esz - 1) // GRAN + 1))
    return name, gr


class Instr:
    __slots__ = ("eng", "fn", "deps", "signal", "value", "is_dma", "sem_key", "idx", "dma_value")

    def __init__(self, eng, fn, is_dma=False, sem_key=None):
        self.eng = eng
        self.fn = fn
        self.deps = set()
        self.signal = False
        self.value = None
        self.is_dma = is_dma
        self.sem_key = sem_key
        self.idx = None
        self.dma_value = None


class Sched:
    def __init__(self, nc):
        self.nc = nc
        self.streams = {e: [] for e in ENGINES}
        self.writer = {}
        self.readers = {}
        self.dma_counts = {}
        self.n_instr = 0

    def _add_dep(self, ins, prod, raw):
        if prod is None or prod is ins:
            return
        if prod.is_dma:
            ins.deps.add(prod)
            return
        if prod.eng == ins.eng and not ins.is_dma:
            if prod.eng == "pe":
                return
        prod.signal = True
        ins.deps.add(prod)

    def _keys(self, items, excl=None):
        out = []
        for it in items:
            if isinstance(it, (str, tuple)):
                out.append(it)
            else:
                name = it.tensor.name
                if name.startswith("pb"):
                    (excl if excl is not None else out).append((name, "bank"))
                    continue
                name, gr = ap_granules(it)
                out.extend((name, g) for g in gr)
        return out

    def op(self, eng, fn, reads=(), writes=(), sem_key=None):
        is_dma = sem_key is not None
        ins = Instr(eng, fn, is_dma=is_dma, sem_key=sem_key)
        ins.idx = len(self.streams[eng])
        wk = self._keys(writes)
        rk = self._keys(reads, excl=wk)
        W = self.writer
        RD = self.readers
        for b in rk:
            self._add_dep(ins, W.get(b), True)
        for b in wk:
            self._add_dep(ins, W.get(b), False)
            rd = RD.get(b)
            if rd:
                for r in rd.values():
                    self._add_dep(ins, r, False)
        for b in rk:
            rd = RD.get(b)
            if rd is None:
                rd = RD[b] = {}
            if is_dma:
                rd[("dma", id(ins))] = ins
            else:
                rd[eng] = ins
        for b in wk:
            W[b] = ins
            RD[b] = {}
        if is_dma:
            v = self.dma_counts.get(sem_key, 0) + 16
            self.dma_counts[sem_key] = v
            ins.dma_value = v
        self.streams[eng].append(ins)
        self.n_instr += 1
        if not is_dma:
            self.last = ins
        return ins

    def mark(self, key):
        self.writer[key] = self.last
        self.readers[key] = {}

    def emit(self, final_dma_waits=()):
        nc = self.nc
        for e in ENGINES:
            c = 0
            for ins in self.streams[e]:
                if ins.is_dma:
                    continue
                if ins.signal:
                    c += 1
                ins.value = c
        with contextlib.ExitStack() as st:
            esem = {e: st.enter_context(nc.semaphore("prog_" + e)) for e in ENGINES}
            dsem = {k: st.enter_context(nc.semaphore("dma_%d" % i)) for i, k in enumerate(self.dma_counts)}
            block = st.enter_context(nc.Block())

            def run_stream(e, eng):
                waited = {}
                for ins in self.streams[e]:
                    need = {}
                    for p in ins.deps:
                        if p.is_dma:
                            key = ("d", p.sem_key)
                            val = p.dma_value
                        else:
                            key = ("e", p.eng)
                            val = p.value
                        if val > need.get(key, 0):
                            need[key] = val
                    for key, val in need.items():
                        if waited.get(key, 0) >= val:
                            continue
                        waited[key] = val
                        sem = dsem[key[1]] if key[0] == "d" else esem[key[1]]
                        eng.wait_ge(sem, val)
                    r = ins.fn(eng)
                    if ins.is_dma:
                        r.then_inc(dsem[ins.sem_key], 16)
                    elif ins.signal:
                        r.then_inc(esem[e], 1)
                for p in final_dma_waits:
                    if p.eng == e:
                        eng.wait_ge(dsem[p.sem_key], p.dma_value)

            @block.sync
            def _(eng):
                run_stream("sp", eng)

            @block.scalar
            def _(eng):
                run_stream("act", eng)

            @block.vector
            def _(eng):
                run_stream("dve", eng)

            @block.gpsimd
            def _(eng):
                run_stream("pool", eng)

            @block.tensor
            def _(eng):
                run_stream("pe", eng)


D = 1024
NCH = 8
SEQ = 2048
CTX = 256
L = CTX + SEQ
DFF = 2816
NJ = 22
NB = 16
NCORES = 8
SPC = NB // NCORES
ALPHA = 4.0 ** 0.25
EPS_P = 1e-5 / (ALPHA * ALPHA)
NEG = -30000.0
ARENA_BYTES = 212736
import os as _os
_ABP = int(_os.environ.get("ABP", "4"))
_ABQ = int(_os.environ.get("ABQ", "9"))

TILES = [(0, CTX, 2)] + [(CTX + 512 * i, 512, None) for i in range(4)]


def _host_consts():
    rows = SEQ // 64
    t = np.arange(SEQ)
    row = (t // 64).astype(np.float32)
    col = (t % 64).astype(np.float32)
    inv = (10000.0 ** (-np.arange(16, dtype=np.float32) / 16)).astype(np.float32)
    ang = np.concatenate([row[:, None] * inv, col[:, None] * inv], axis=-1).astype(np.float32)
    cos = np.cos(ang).astype(np.float32)
    sin = np.sin(ang).astype(np.float32)
    cs = np.zeros((128, 2, SEQ), np.float32)
    for p in range(128):
        d = p % 64
        j = d % 32
        cs[p, 0] = cos[:, j]
        cs[p, 1] = (-sin[:, j]) if d < 32 else sin[:, j]
    i = np.arange(128)[:, None]
    j = np.arange(128)[None, :]
    m_prev = np.where(j <= i, 0.0, NEG).astype(np.float32)
    m_next = np.where(i <= j, 0.0, NEG).astype(np.float32)
    masks = np.stack([np.tile(m_prev, (1, 4)), np.tile(m_next, (1, 4))], axis=1)
    pm = np.zeros((128, 4, 5, 128), np.float32)
    for g, w in enumerate((2, 4, 8, 16)):
        r = w // 2
        n = 3 * 128
        full_mid = np.zeros((n, n), np.float64)
        for tt in range(n):
            lo, hi = max(tt - r, 0), min(tt + r, n - 1)
            full_mid[lo:hi + 1, tt] = 1.0 / (hi - lo + 1)
            full_mid[tt, tt] -= 1.0
        pm[:, g, 0] = full_mid[0:128, 128:256]
        pm[:, g, 2] = full_mid[128:256, 128:256]
        pm[:, g, 4] = full_mid[256:384, 128:256]
        pm[:, g, 1] = full_mid[0:128, 0:128]
        pm[:, g, 3] = full_mid[256:384, 256:384]
    ident = np.eye(128, dtype=np.float32)
    bf = ml_dtypes.bfloat16
    return {
        "k_cs": cs,
        "k_masks": masks.astype(bf),
        "k_pm": pm.astype(bf),
        "k_ident": ident,
        "k_identb": ident.astype(bf),
    }


def build(stop_after=None):
    nc = bass.Bass("TRN2", target_bir_lowering=False)

    def din(name, shape, dt=F32):
        return nc.dram_tensor(name, list(shape), dt, kind="ExternalInput")

    def dscr(name, shape, dt=BF):
        return nc.dram_tensor(name, list(shape), dt, kind="Internal")

    x_d = din("x", [SPC, SEQ, D])
    ctx_d = din("ctx", [SPC, CTX, D])
    cv_d = din("cvec", [128, NCH, 3])
    wmod_d = din("w_mod", [2, D, 9 * D])
    bmod_d = din("b_mod3", [2, 128, 72, 3])
    lng_d = din("ln_g", [128, 2, 3, NCH])
    lnb_d = din("ln_b", [128, 2, 3, NCH])
    wg_d = din("ffn_w_gate", [2, 2, D, DFF])
    wu_d = din("ffn_w_up", [2, 2, D, DFF])
    wd_d = din("ffn_w_down", [2, 2, DFF, D])
    win_d = din("mix_ab_w_in", [1, D, 1280])
    sink_d = din("attn_sink", [1, 8])
    pw_d = din("pool_w", [1, 4, 128, 128])
    psc_d = din("pool_scale", [128, 4])
    wout_d = din("mix_ab_w_out", [1, D, D])
    lin_d = din("lru_w_in", [1, D, 2 * D])
    cw_d = din("lru_conv_w", [128, NCH, 4])
    cb_d = din("lru_conv_b", [128, NCH])
    wa_d = din("lru_wa", [1, 2, 8, 128, 128])
    ba_d = din("lru_ba", [128, 2, NCH])
    wx_d = din("lru_wx", [1, 2, 8, 128, 128])
    bx_d = din("lru_bx", [128, 2, NCH])
    lam_d = din("lru_lambda", [128, 2, NCH])
    lout_d = din("lru_w_out", [1, D, D])
    kcs_d = din("k_cs", [128, 2, SEQ])
    kmask_d = din("k_masks", [128, 2, 512], BF)
    kpm_d = din("k_pm", [128, 4, 5, 128], BF)
    kid_d = din("k_ident", [128, 128])
    kidb_d = din("k_identb", [128, 128], BF)
    out_d = nc.dram_tensor("out", [SPC, SEQ, D], F32, kind="ExternalOutput")

    s_g = [[dscr("s_g%d%d" % (l, f), [D, DFF]) for f in range(2)] for l in range(2)]
    s_u = [[dscr("s_u%d%d" % (l, f), [D, DFF]) for f in range(2)] for l in range(2)]
    s_d = [[dscr("s_d%d%d" % (l, f), [DFF, D]) for f in range(2)] for l in range(2)]
    s_qk = dscr("s_qk", [D, 1280])
    s_qks = dscr("s_qks", [D, 640])
    s_pw = dscr("s_pw", [4, 128, 128])
    s_wo = dscr("s_wo", [D, D])
    s_lin = dscr("s_lin", [D, 2 * D])
    s_wa = dscr("s_wa", [2, 8, 128, 128])
    s_wx = dscr("s_wx", [2, 8, 128, 128])
    s_lo = dscr("s_lo", [D, D])

    st = contextlib.ExitStack()
    arena = st.enter_context(nc.sbuf_tensor("arena", [128, ARENA_BYTES // 2], BF))
    pb = [st.enter_context(nc.psum_tensor("pb%d" % i, [128, 512], F32)) for i in range(8)]
    S = Sched(nc)

    def V(off, shape, dt):
        n = int(np.prod(shape))
        esz = 4 if dt == F32 else 2
        assert off % 256 == 0 or True
        assert off + n * esz <= ARENA_BYTES, (off, shape)
        v = arena[:, off // 2:(off + n * esz) // 2]
        if dt == F32:
            v = v.bitcast(F32)
        if len(shape) == 2:
            v = v.rearrange("p (a b) -> p a b", a=shape[0])
        elif len(shape) == 3:
            v = v.rearrange("p (a b c) -> p a b c", a=shape[0], b=shape[1])
        return v

    o = 0
    h = V(o, [NCH, L], F32); o += NCH * L * 4
    ident = V(o, [128], F32); o += 512
    identb = V(o, [128], BF); o += 256
    onesN = V(o, [128], BF); o += 256
    onesA = V(o, [128], BF); o += 256
    onesB = V(o, [128], BF); o += 256
    masks = V(o, [2, 512], BF); o += 2048
    modv = [V(o + l * 1024, [72, 3], F32) for l in range(2)]; o += 2048
    lng = V(o, [2, 3, NCH], F32); o += 256
    lnb = V(o, [2, 3, NCH], F32); o += 256
    cvals = V(o, [4], F32); o += 256
    cvt = V(o, [NCH, 3], F32); o += 256
    expsink = V(o, [8], F32); o += 256
    psc = V(o, [4], F32); o += 256
    RB = o
    assert RB % 256 == 0

    def RV(off, shape, dt):
        return V(RB + off, shape, dt)

    r_bf = RV(0, [NCH, 512], BF)
    rsq_bf = RV(8192, [NCH, 512], BF)
    mean_sb = RV(16384, [512], F32)
    var_sb = RV(18432, [512], F32)
    rstd_sb = RV(20480, [512], F32)
    tn = [RV(22528 + 2048 * i, [512], F32) for i in range(2)]
    sg = [RV(26624 + 2048 * i, [512], F32) for i in range(2)]
    xmod = [RV(30720 + 8192 * i, [NCH, 512], BF) for i in range(2)]
    PH = 47104

    cnt = {"dma": 0}

    def MM(out, lhsT, rhs, start, stop):
        S.op("pe", lambda e: e.matmul(out, lhsT=lhsT, rhs=rhs, start=start, stop=stop), reads=[lhsT, rhs], writes=[out])

    def TR(out, in_, idn):
        S.op("pe", lambda e: e.transpose(out, in_, idn), reads=[in_, idn], writes=[out])

    def ACTV(out, in_, func, scale=1.0, bias=None, eng="act"):
        rd = [in_]
        kw = {}
        if not isinstance(scale, (int, float)):
            rd.append(scale)
        if bias is not None:
            if not isinstance(bias, (int, float)):
                rd.append(bias)
            kw["bias"] = bias
        S.op("act", lambda e: e.activation(out=out, in_=in_, func=func, scale=scale, **kw), reads=rd, writes=[out])

    def TT(eng, out, in0, in1, op):
        S.op(eng, lambda e: e.tensor_tensor(out=out, in0=in0, in1=in1, op=op), reads=[in0, in1], writes=[out])

    def TS(eng, out, in0, s1, op0, s2=None, op1=None):
        rd = [in0] + [s for s in (s1, s2) if s is not None and not isinstance(s, (int, float))]
        if op1 is None:
            S.op(eng, lambda e: e.tensor_scalar(out=out, in0=in0, scalar1=s1, scalar2=None, op0=op0), reads=rd, writes=[out])
        else:
            S.op(eng, lambda e: e.tensor_scalar(out=out, in0=in0, scalar1=s1, scalar2=s2, op0=op0, op1=op1), reads=rd, writes=[out])

    def STT(eng, out, in0, scalar, in1, op0, op1):
        rd = [in0, in1] + ([] if isinstance(scalar, (int, float)) else [scalar])
        S.op(eng, lambda e: e.scalar_tensor_tensor(out=out, in0=in0, scalar=scalar, in1=in1, op0=op0, op1=op1), reads=rd, writes=[out])

    def CP(eng, out, in_):
        if eng == "act":
            S.op("act", lambda e: e.activation(out=out, in_=in_, func=AF.Copy), reads=[in_], writes=[out])
        else:
            S.op(eng, lambda e: e.tensor_copy(out=out, in_=in_), reads=[in_], writes=[out])

    def MEMSET(eng, out, val):
        S.op(eng, lambda e: e.memset(out, val), writes=[out])

    def DMA(eng, out, in_, reads=(), writes=(), key=None, **kw):
        if key is None:
            cnt["dma"] += 1
            key = ("u", cnt["dma"])
        return S.op(eng, lambda e: e.dma_start(out=out, in_=in_, **kw), reads=list(reads), writes=list(writes), sem_key=key)

    def dap(t, offset, pat):
        return bass.AP(t, offset, [list(p) for p in pat])

    gate = {"k": None, "n": 0}

    def new_gate():
        gate["n"] += 1
        gate["k"] = ("gate", gate["n"])
        S.mark(gate["k"])

    def grd():
        return [gate["k"]] if gate["k"] is not None else []

    def cast2d(dst_t, src_t, src_off, nelem, rowlen, key):
        nrow = nelem // rowlen
        DMA("pool", dap(dst_t, 0, [[rowlen, nrow], [1, rowlen]]), dap(src_t, src_off, [[rowlen, nrow], [1, rowlen]]), reads=grd(), writes=[key], key=("c", key))

    castq = []

    def cast_ffn(l, f, tag=None):
        off = (l * 2 + f) * D * DFF
        first = (l, f) == (0, 0)
        pcs = []
        for g in range(11):
            for (dst_t, src_t, nm) in ((s_g[l][f], wg_d, "s_g"), (s_u[l][f], wu_d, "s_u")):
                wk = [("s_g", l, f, g), ("s_u", l, f, g)] if first else [(nm, l, f)]
                sk = ("c", "gu", l, f, g) if first else ("c", nm, l, f)
                pcs.append(lambda dst_t=dst_t, src_t=src_t, wk=wk, sk=sk, g=g: DMA(
                    "pool", dap(dst_t, g * 256, [[DFF, D], [1, 256]]), dap(src_t, off + g * 256, [[DFF, D], [1, 256]]),
                    reads=grd(), writes=wk, key=sk))
        for dp in range(4):
            wk = ("s_d", l, f, dp) if first else ("s_d", l, f)
            sk = ("c", "s_d", l, f, dp) if first else ("c", "s_d", l, f)
            pcs.append(lambda wk=wk, sk=sk, dp=dp: DMA(
                "pool", dap(s_d[l][f], dp * 256, [[D, DFF], [1, 256]]), dap(wd_d, off + dp * 256, [[D, DFF], [1, 256]]),
                reads=grd(), writes=[wk], key=sk))
        if tag is None:
            for p_ in pcs:
                p_()
        else:
            castq.extend((tag, p_) for p_ in pcs)

    def tick(n=1):
        for _ in range(n):
            if castq:
                new_gate()
                castq.pop(0)[1]()

    def flush_tag(tag):
        while any(t == tag for t, _ in castq):
            new_gate()
            castq.pop(0)[1]()

    def cast_ab():
        for kvh in range(2):
            DMA("pool", dap(s_qk, kvh * 64, [[1280, D], [128, 4], [1, 64]]),
                dap(win_d, kvh * 256, [[1280, D], [64, 4], [1, 64]]), reads=grd(), writes=[("s_qk", "q", kvh)], key=("c", "qk", kvh))
        DMA("pool", dap(s_qk, 512, [[1280, D], [1, 768]]), dap(win_d, 512, [[1280, D], [1, 768]]), reads=grd(), writes=[("s_qk", "r")], key=("c", "qk", 2))
        for kvh in range(2):
            for half in range(2):
                DMA("pool", dap(s_qks, kvh * 64 + half * 32, [[640, D], [128, 4], [1, 32]]),
                    dap(win_d, kvh * 256 + (1 - half) * 32, [[1280, D], [64, 4], [1, 32]]),
                    reads=grd(), writes=[("s_qks", kvh, half)], key=("c", "qks", kvh, half))
        for half in range(2):
            DMA("pool", dap(s_qks, 512 + half * 32, [[640, D], [64, 2], [1, 32]]),
                dap(win_d, 512 + (1 - half) * 32, [[1280, D], [64, 2], [1, 32]]),
                reads=grd(), writes=[("s_qks", "k", half)], key=("c", "qks", "k", half))
        cast2d(s_pw, pw_d, 0, 4 * 128 * 128, 2048, "s_pw")
        cast2d(s_wo, wout_d, 0, D * D, 1024, "s_wo")

    def cast_lru():
        cast2d(s_lin, lin_d, 0, D * 2 * D, 2048, "s_lin")
        cast2d(s_wa, wa_d, 0, 2 * 8 * 128 * 128, 2048, "s_wa")
        cast2d(s_wx, wx_d, 0, 2 * 8 * 128 * 128, 2048, "s_wx")
        cast2d(s_lo, lout_d, 0, D * D, 1024, "s_lo")

    AB_KEYS = [("s_qk", "q", 0), ("s_qk", "q", 1), ("s_qk", "r")] + [("s_qks", a, b) for a in (0, 1, "k") for b in (0, 1)]

    DMA("sp", ident, kid_d.ap(), writes=[ident])
    DMA("sp", identb, kidb_d.ap(), writes=[identb])
    DMA("sp", masks, kmask_d.ap(), writes=[masks])
    DMA("sp", lng, lng_d.ap(), writes=[lng])
    DMA("sp", lnb, lnb_d.ap(), writes=[lnb])
    DMA("sp", cvt, cv_d.ap(), writes=[cvt])
    DMA("sp", expsink, sink_d.ap()[0].partition_broadcast(128), writes=[expsink])
    MEMSET("dve", onesN, 1.0 / 1024.0)
    MEMSET("dve", onesA[:, 0:64], 1.0)
    MEMSET("dve", onesA[:, 64:128], 0.0)
    MEMSET("dve", onesB[:, 0:64], 0.0)
    MEMSET("dve", onesB[:, 64:128], 1.0)
    MEMSET("dve", cvals[:, 0:1], EPS_P)
    MEMSET("dve", cvals[:, 1:2], 1.0)
    MEMSET("dve", cvals[:, 2:3], 0.0)
    cast_ffn(0, 0)

    ACTV(cvt, cvt, AF.Silu)
    bm = RV(PH + 2 * 8192, [72, 3], F32)
    def mod_finalize(l, mps, bm_):
        TT("dve", modv[l], mps, bm_, ALU.add)
        for i in range(3):
            sc = modv[l][:, (3 * i + 1) * 8:(3 * i + 2) * 8, :]
            TS("dve", sc, sc, 1.0, ALU.add)
            gt = modv[l][:, (3 * i + 2) * 8:(3 * i + 3) * 8, :]
            TS("dve", gt, gt, (1.0 if i == 1 else 0.5) / ALPHA, ALU.mult)

    def mod_block(l, b, slot, msb, bank, skey):
        DMA("sp", slot, wmod_d.ap()[l].rearrange("(kc p) n -> p kc n", p=128)[:, :, b * 256:(b + 1) * 256], writes=[slot], key=skey)
        for kc in range(NCH):
            MM(bank[0:3, 256:512], cvt[:, kc, :], slot[:, kc, :], kc == 0, kc == NCH - 1)
        CP("dve", msb[0:3, 0:256], bank[0:3, 256:512])
        for q in range(2):
            fc = 2 * b + q
            TR(bank[:, fc * 3:fc * 3 + 3], msb[0:3, q * 128:(q + 1) * 128], ident[0:3, 0:3])

    wm0 = [RV(PH + i * 8192, [NCH, 256], F32) for i in range(2)]
    DMA("sp", bm, bmod_d.ap()[0], writes=[bm])
    for b in range(36):
        mod_block(0, b, wm0[b % 2], sg[0], pb[3], ("wm", b % 2))
    mod_finalize(0, pb[3][:, 0:216].rearrange("p (a b) -> p a b", b=3), bm)

    def mod1_step(k, SBo):
        slot = RV(SBo + 10240, [NCH, 256], F32)
        bm1 = sg[1][:, 0:216].rearrange("p (a b) -> p a b", b=3)
        if k == 0:
            DMA("sp", bm1, bmod_d.ap()[1], writes=[bm1])
        for b in (2 * k, 2 * k + 1):
            mod_block(1, b, slot, sg[0], pb[3], ("wm1", 0))
        if k == 17:
            mod_finalize(1, pb[3][:, 0:216].rearrange("p (a b) -> p a b", b=3), bm1)

    def mvec(l, v, ch, col):
        return modv[l][:, v * 8 + ch, col:col + 1]

    ACTV(expsink, expsink, AF.Exp)

    def modulate(dst, l, i, c0, T, col, dcol0=0):
        for ch in range(NCH):
            ACTV(dst[:, ch, dcol0:dcol0 + T], h[:, ch, c0:c0 + T], AF.Identity, scale=mvec(l, 3 * i + 1, ch, col), bias=mvec(l, 3 * i, ch, col))

    def resid(ch, yps, l, i, c0, T, col):
        hs = h[:, ch, c0:c0 + T]
        STT("dve", hs, yps, mvec(l, 3 * i + 2, ch, col), hs, ALU.mult, ALU.add)
        CP("act", r_bf[:, ch, :T], hs)
        ACTV(rsq_bf[:, ch, :T], hs, AF.Square)

    def layernorm(l, i, c0, T):
        for ch in range(NCH):
            MM(pb[6][:, :T], onesN, r_bf[:, ch, :T], ch == 0, ch == NCH - 1)
        for ch in range(NCH):
            MM(pb[7][:, :T], onesN, rsq_bf[:, ch, :T], ch == 0, ch == NCH - 1)
        CP("act", mean_sb[:, :T], pb[6][:, :T])
        TT("dve", var_sb[:, :T], mean_sb[:, :T], mean_sb[:, :T], ALU.mult)
        TT("dve", var_sb[:, :T], pb[7][:, :T], var_sb[:, :T], ALU.subtract)
        ACTV(rstd_sb[:, :T], var_sb[:, :T], AF.Sqrt, bias=cvals[:, 0:1])
        S.op("dve", lambda e: e.reciprocal(out=rstd_sb[:, :T], in_=rstd_sb[:, :T]), reads=[rstd_sb[:, :T]], writes=[rstd_sb[:, :T]])
        def mk(ch):
            def _f():
                t_ = tn[ch % 2]
                hs = h[:, ch, c0:c0 + T]
                TT("dve", t_[:, :T], hs, mean_sb[:, :T], ALU.subtract)
                TT("dve", t_[:, :T], t_[:, :T], rstd_sb[:, :T], ALU.mult)
                ACTV(hs, t_[:, :T], AF.Identity, scale=lng[:, l, i, ch:ch + 1], bias=lnb[:, l, i, ch:ch + 1])
            return _f
        for ch in range(NCH):
            pending.append(mk(ch))

    pending = []

    def drain(n=None):
        k = len(pending) if n is None else min(n, len(pending))
        for _ in range(k):
            pending.pop(0)()

    hid = RV(PH, [NJ, 512], BF)
    wA = [[RV(PH + 22528 + s * 8192 + t * 4096, [NCH, 256], BF) for t in range(2)] for s in range(3)]
    wB = [RV(PH + 22528 + 24576 + s * 11264, [NJ, 256], BF) for s in range(2)]

    def ffn(l, f, seq, tiles, hooks=None):
        i = 0 if f == 0 else 2
        gsrc = s_g[l][f].ap().rearrange("(kc p) n -> p kc n", p=128)
        usrc = s_u[l][f].ap().rearrange("(kc p) n -> p kc n", p=128)
        dsrc = s_d[l][f].ap().rearrange("(j p) n -> p j n", p=128)
        steps = []
        for ti in range(len(tiles)):
            for g in range(11):
                steps.append(("A", ti, g))
            for dp in range(4):
                steps.append(("B", ti, dp))
        ia = {"A": 0, "B": 0}
        jcount = 0
        for k in range(len(steps)):
            kind, ti, g = steps[k]
            if seq == 0 and k % 3 == 2:
                tick()
            if kind == "A":
                s = ia["A"] % 3
                ia["A"] += 1
                DMA("sp", wA[s][0], gsrc[:, :, g * 256:(g + 1) * 256], reads=[("s_g", l, f, g) if (l, f) == (0, 0) else ("s_g", l, f)], writes=[wA[s][0]], key=("wA", s, 0))
                DMA("sp", wA[s][1], usrc[:, :, g * 256:(g + 1) * 256], reads=[("s_u", l, f, g) if (l, f) == (0, 0) else ("s_u", l, f)], writes=[wA[s][1]], key=("wA", s, 1))
            else:
                s = ia["B"] % 2
                ia["B"] += 1
                DMA("sp", wB[s], dsrc[:, :, g * 256:(g + 1) * 256], reads=[("s_d", l, f, g) if (l, f) == (0, 0) else ("s_d", l, f)], writes=[wB[s]], key=("wB", s))
            kind, ti, g = steps[k]
            c0, T, col = tiles[ti]
            if col is None:
                col = seq
            xm = xmod[ti % 2]
            if kind == "A":
                if g == 0 and ti == 0:
                    modulate(xm, l, i, c0, T, col)
                for jj in range(2):
                    j = 2 * g + jj
                    ga = pb[jcount % 2]
                    ub = pb[2 + jcount % 2]
                    sgt = sg[jcount % 2]
                    jcount += 1
                    for kc in range(NCH):
                        MM(ga[:, :T], wA[s][0][:, kc, jj * 128:(jj + 1) * 128], xm[:, kc, :T], kc == 0, kc == NCH - 1)
                    for kc in range(NCH):
                        MM(ub[:, :T], wA[s][1][:, kc, jj * 128:(jj + 1) * 128], xm[:, kc, :T], kc == 0, kc == NCH - 1)
                    ACTV(sgt[:, :T], ga[:, :T], AF.Silu)
                    TT("dve", hid[:, j, :T], sgt[:, :T], ub[:, :T], ALU.mult)
                    drain(1)
            else:
                if g == 0 and ti + 1 < len(tiles):
                    c0n, Tn, coln = tiles[ti + 1]
                    drain()
                    modulate(xmod[(ti + 1) % 2], l, i, c0n, Tn, seq if coln is None else coln)
                for dd in range(2):
                    ch = 2 * g + dd
                    yps = pb[4 + dd]
                    for j in range(NJ):
                        MM(yps[:, :T], wB[s][:, j, dd * 128:(dd + 1) * 128], hid[:, j, :T], j == 0, j == NJ - 1)
                    resid(ch, yps[:, :T], l, i, c0, T, col)
                if g == 3:
                    drain()
                    layernorm(l, 0 if f == 0 else 2, c0, T)
                    if hooks and ti in hooks:
                        hooks[ti]()

    stg = [RV(PH + i * 4096, [D], F32) for i in range(2)]

    def load_seq(seq):
        for blk in range(L // 128):
            sl = stg[blk % 2]
            if blk < 2:
                src = ctx_d.ap()[seq, blk * 128:(blk + 1) * 128, :]
            else:
                src = x_d.ap()[seq, (blk - 2) * 128:(blk - 1) * 128, :]
            DMA("sp", sl, src, writes=[sl], key=("stg", blk % 2))
            for half in range(2):
                bank = pb[(blk * 2 + half) % 4]
                for q in range(4):
                    ch = half * 4 + q
                    TR(bank[:, q * 128:(q + 1) * 128], sl[:, ch * 128:(ch + 1) * 128], ident)
                dst = h[:, half * 4:half * 4 + 4, blk * 128:(blk + 1) * 128]
                srcp = bank[:].rearrange("p (a b) -> p a b", b=128)
                CP("act" if half == 0 else "dve", dst, srcp)

    out_dmas = []

    def store_seq(seq):
        for blk in range(SEQ // 128):
            sl = stg[blk % 2]
            c0 = CTX + blk * 128
            for half in range(2):
                bank = pb[(blk * 2 + half) % 4]
                for q in range(4):
                    ch = half * 4 + q
                    TR(bank[:, q * 128:(q + 1) * 128], h[:, ch, c0:c0 + 128], ident)
                CP("act" if half == 0 else "dve", sl[:, half * 512:(half + 1) * 512], bank[:])
            out_dmas.append(DMA("sp", out_d.ap()[seq, blk * 128:(blk + 1) * 128, :], sl, reads=[sl], key=("ost", blk % 2)))

    q_rot = RV(PH, [4, L], BF)
    kTk = [RV(PH + 18432 + i * 4608, [L], BF) for i in range(2)]
    vtk = [RV(PH + 27648 + i * 4608, [L // 128, 128], BF) for i in range(2)]
    poolT = RV(PH + 36864, [4, L], BF)
    SB = PH + 55296
    w_qkv = RV(SB, [NCH, 1408], BF)
    csb = [RV(i * 4096, [2, 512], F32) for i in range(2)]
    rtmp = [RV(8192 + i * 2048, [512], F32) for i in range(2)]
    w_u = RV(SB, [NCH, 512], BF)
    utok = RV(SB + 8192, [8, 512], BF)
    w_pool = RV(SB + 16384, [4, 128], BF)
    pmt = RV(SB + 17408, [4, 5, 128], BF)
    dTt = [RV(SB + 22528 + i * 1024, [4, 128], BF) for i in range(2)]
    pT = [RV(SB + i * 1024, [512], BF) for i in range(4)]
    den = [RV(SB + 4096 + i * 2048, [512], F32) for i in range(2)]
    sinkrow = RV(SB + 8192, [512], F32)
    w_o = RV(SB, [8, D], BF)

    def mixer_ab(seq):
        l = 0
        src = s_qk.ap().rearrange("(kc p) n -> p kc n", p=128)
        srcs = s_qks.ap().rearrange("(kc p) n -> p kc n", p=128)
        DMA("sp", w_qkv[:, :, 0:768], src[:, :, 0:768], reads=AB_KEYS, writes=[w_qkv[:, :, 0:768]])
        DMA("sp", w_qkv[:, :, 768:1408], srcs, reads=AB_KEYS, writes=[w_qkv[:, :, 768:1408]])
        MEMSET("dve", kTk[0][64:128, :], 0.0)
        MEMSET("dve", kTk[1][0:64, :], 0.0)
        MEMSET("dve", vtk[0][:, :, 64:128], 0.0)
        MEMSET("dve", vtk[1][:, :, 0:64], 0.0)
        for ti, (c0, T, col) in enumerate(TILES):
            if col is None:
                col = seq
            xm = xmod[ti % 2]
            lat = ti > 0
            if lat:
                cs_ = csb[ti % 2]
                DMA("sp", cs_, kcs_d.ap()[:, :, (ti - 1) * 512:ti * 512], writes=[cs_], key=("cs", ti % 2))
                drain()
            modulate(xm, l, 1, c0, T, col)
            for c in range(5):
                drain(2)
                pq = pb[(c % 2) * 2]
                pqs = pb[(c % 2) * 2 + 1]
                wc = c * 128
                dst = q_rot[:, c, c0:c0 + T] if c < 4 else None
                for kc in range(NCH):
                    MM(pq[:, :T], w_qkv[:, kc, wc:wc + 128], xm[:, kc, :T], kc == 0, kc == NCH - 1)
                if lat:
                    for kc in range(NCH):
                        MM(pqs[:, :T], w_qkv[:, kc, 768 + wc:768 + wc + 128], xm[:, kc, :T], kc == 0, kc == NCH - 1)
                    TT("dve", rtmp[0][:, :T], pq[:, :T], cs_[:, 0, :T], ALU.mult)
                    TT("dve", rtmp[1][:, :T], pqs[:, :T], cs_[:, 1, :T], ALU.mult)
                    if c < 4:
                        TT("dve", dst, rtmp[0][:, :T], rtmp[1][:, :T], ALU.add)
                    else:
                        for hv in range(2):
                            pr_ = slice(hv * 64, hv * 64 + 64)
                            TT("dve", kTk[hv][pr_, c0:c0 + T], rtmp[0][pr_, :T], rtmp[1][pr_, :T], ALU.add)
                else:
                    if c < 4:
                        CP("act", dst, pq[:, :T])
                    else:
                        for hv in range(2):
                            pr_ = slice(hv * 64, hv * 64 + 64)
                            CP("act", kTk[hv][pr_, c0:c0 + T], pq[pr_, :T])
            nblk = T // 128
            for b in range(nblk):
                for kc in range(NCH):
                    MM(pb[4][:, b * 128:(b + 1) * 128], xm[:, kc, b * 128:(b + 1) * 128], w_qkv[:, kc, 640:768], kc == 0, kc == NCH - 1)
            vps = pb[4][:, :T].rearrange("p (a b) -> p a b", b=128)
            CP("act", vtk[0][:, c0 // 128:c0 // 128 + nblk, 0:64], vps[:, :, 0:64])
            CP("dve", vtk[1][:, c0 // 128:c0 // 128 + nblk, 64:128], vps[:, :, 64:128])
        if _ABP < 2:
            return
        DMA("sp", w_u, src[:, :, 768:1280], reads=AB_KEYS, writes=[w_u])
        DMA("sp", w_pool, s_pw.ap().rearrange("g c e -> c g e"), reads=["s_pw"], writes=[w_pool])
        DMA("sp", pmt, kpm_d.ap(), writes=[pmt])
        DMA("sp", psc, psc_d.ap(), writes=[psc])

        def ring(cb):
            return (cb + 6) % 8 if cb < 2 else (cb - 2) % 8

        def pool_block(cb):
            if _ABQ < 2:
                return
            first = cb in (0, 2)
            last = cb in (1, 17)
            bank = pb[2 + cb % 2]
            for g in range(4):
                terms = []
                if not first:
                    terms.append((cb - 1, 0))
                terms.append((cb, 1 if first else (3 if last else 2)))
                if not last:
                    terms.append((cb + 1, 4))
                for n_, (sb_, var) in enumerate(terms):
                    MM(bank[:, g * 128:(g + 1) * 128], utok[:, ring(sb_), g * 128:(g + 1) * 128], pmt[:, g, var, :], n_ == 0, n_ == len(terms) - 1)
            dt_ = dTt[cb % 2]
            CP("dve", dt_, bank[:].rearrange("p (a b) -> p a b", b=128))
            if _ABQ < 3:
                return
            bank2 = pb[4 + cb % 2]
            for g in range(4):
                MM(bank2[:, g * 128:(g + 1) * 128], w_pool[:, g, :], dt_[:, g, :], True, True)
            for g in range(4):
                ACTV(poolT[:, g, cb * 128:(cb + 1) * 128], bank2[:, g * 128:(g + 1) * 128], AF.Identity, scale=psc[:, g:g + 1])

        for ti, (c0, T, col) in enumerate(TILES):
            if col is None:
                col = seq
            xm = xmod[ti % 2]
            if _ABQ < 1:
                continue
            modulate(xm, l, 1, c0, T, col)
            nblk = T // 128
            for b in range(nblk):
                cb = c0 // 128 + b
                bank = pb[cb % 2]
                for kc in range(NCH):
                    MM(bank[:], xm[:, kc, b * 128:(b + 1) * 128], w_u[:, kc, :], kc == 0, kc == NCH - 1)
                CP("act", utok[:, ring(cb), :], bank[:])
            if ti == 0:
                pool_block(0)
                pool_block(1)
            else:
                b0 = c0 // 128
                if ti > 1:
                    pool_block(b0 - 1)
                for b in range(3):
                    pool_block(b0 + b)
                if ti == 4:
                    pool_block(b0 + 3)
        if _ABP < 3:
            return
        for kvh in range(2):
            for g in range(4):
                hd = kvh * 4 + g
                sl = sinkrow[kvh * 64:(kvh + 1) * 64, g * 128:(g + 1) * 128]
                ACTV(sl, ident[kvh * 64:(kvh + 1) * 64, :], AF.Identity, scale=0.0, bias=expsink[kvh * 64:(kvh + 1) * 64, hd:hd + 1])
        items = []
        for qb in range(L // 128):
            if qb < 2:
                keys = [(0, None), (1, None)]
            else:
                keys = []
                if qb - 1 >= 2:
                    keys.append((qb - 1, 0))
                keys.append((qb, None))
                if qb + 1 <= 17:
                    keys.append((qb + 1, 1))
                keys += [(0, None), (1, None)]
            for kvh in range(2):
                for n_, (kb, mk) in enumerate(keys):
                    items.append((qb, kvh, kb, mk, kvh == 0 and n_ == 0, kvh == 1 and n_ == len(keys) - 1))
        NI = len(items)
        LA = 2
        for t in range(NI + LA):
            if t < NI:
                qb, kvh, kb, mk, fst, lst = items[t]
                qc = qb * 128
                sps = pb[t % 3]
                ptt = pT[t % 4]
                MM(sps[:], kTk[kvh][:, kb * 128:(kb + 1) * 128], q_rot[:, :, qc:qc + 128], True, mk is None)
                if mk is not None:
                    MM(sps[:], identb, masks[:, mk, :], False, True)
                ACTV(ptt, sps[:], AF.Exp, scale=0.125)
            u_ = t - LA
            if u_ >= 0:
                qb, kvh, kb, mk, fst, lst = items[u_]
                qc = qb * 128
                ptt = pT[u_ % 4]
                ops_ = pb[4 + qb % 2]
                dps_ = pb[6 + qb % 2]
                MM(ops_[:], vtk[kvh][:, kb, :], ptt, fst, lst)
                MM(dps_[:], (onesA if kvh == 0 else onesB), ptt, fst, lst)
                if lst:
                    if seq == 0:
                        mod1_step(qb, SB)
                        tick()
                    dn = den[qb % 2]
                    TT("dve", dn, dps_[:], sinkrow, ALU.add)
                    S.op("dve", lambda e, dn=dn: e.reciprocal(out=dn, in_=dn), reads=[dn], writes=[dn])
                    TT("dve", q_rot[:, :, qc:qc + 128], ops_[:].rearrange("p (a b) -> p a b", b=128), dn.rearrange("p (a b) -> p a b", b=128), ALU.mult)
        if _ABP < 4:
            return
        wo_s = s_wo.ap()
        for two in range(2):
            DMA("sp", w_o[two * 64:(two + 1) * 64, 0:4, :], dap(s_wo, two * 256 * D, [[D, 64], [64 * D, 4], [1, D]]), reads=["s_wo"], writes=[w_o[:, 0:4, :]])
        DMA("sp", w_o[:, 4:8, :], wo_s[512:1024, :].rearrange("(g p) n -> p g n", p=128), reads=["s_wo"], writes=[w_o[:, 4:8, :]])
        for ti, (c0, T, col) in enumerate(TILES):
            if col is None:
                col = seq
            for ch in range(NCH):
                yps = pb[ch % 2]
                for g in range(4):
                    MM(yps[:, :T], w_o[:, g, ch * 128:(ch + 1) * 128], q_rot[:, g, c0:c0 + T], g == 0, False)
                for g in range(4):
                    MM(yps[:, :T], w_o[:, 4 + g, ch * 128:(ch + 1) * 128], poolT[:, g, c0:c0 + T], False, g == 3)
                resid(ch, yps[:, :T], l, 1, c0, T, col)
                drain(1)
            drain()
            layernorm(l, 1, c0, T)

    LA = RV(0, [L], F32)
    LBb = RV(9216, [L], F32)
    wax = [RV(23040 + i * 1024, [4, 128], BF) for i in range(2)]
    lpar = RV(43776, [5, 2, NCH], F32)
    cwt = RV(43776 + 512, [NCH, 4], F32)
    cbt = RV(43776 + 768, [NCH], F32)
    xfull = RV(PH, [NCH, L], BF)
    gy = RV(PH + 36864, [4, SEQ], BF)
    UB = PH + 36864 + 16384
    UU = [(RV(25088, [2368], F32), RV(34560, [L], F32), RV(18432, [L], BF)),
          (RV(UB, [2368], F32), RV(UB + 9472, [L], F32), RV(UB + 18688, [L], BF))]
    w_loh = RV(UB + 9472, [4, D], BF)
    wsl = [RV(UB + 23296 + i * 4096, [NCH, 2, 128], BF) for i in range(2)]
    CO = 1
    LO = 260

    def mixer_lru(seq):
        l = 1
        DMA("sp", lpar[:, 0], ba_d.ap(), writes=[lpar[:, 0]])
        DMA("sp", lpar[:, 1], bx_d.ap(), writes=[lpar[:, 1]])
        DMA("sp", lpar[:, 2], lam_d.ap(), writes=[lpar[:, 2]])
        DMA("sp", cwt, cw_d.ap(), writes=[cwt])
        DMA("sp", cbt, cb_d.ap(), writes=[cbt])
        ACTV(lpar[:, 2], lpar[:, 2], AF.Exp, scale=-1.0)
        ACTV(lpar[:, 2], lpar[:, 2], AF.Ln, bias=cvals[:, 1:2])
        TS("dve", lpar[:, 2], lpar[:, 2], -8.0, ALU.mult)
        for ti, (c0, T, col) in enumerate(TILES):
            if col is None:
                col = seq
            if ti == 1:
                drain()
            modulate(xfull, l, 1, c0, T, col, dcol0=c0)
        lsrc = s_lin.ap().rearrange("(kc p) n -> p kc n", p=128)
        bk = {"n": 0}

        def nbank():
            bk["n"] += 1
            return pb[2 + bk["n"] % 6]

        def front_a(c):
            U1, U2, ubf = UU[c % 2]
            ws = wsl[c % 2]
            wx_ = wax[c % 2]
            DMA("sp", ws[:, :, 0, :], lsrc[:, :, c * 128:(c + 1) * 128], reads=["s_lin"], writes=[ws[:, :, 0, :]], key=("wsl", c % 2, 0))
            DMA("sp", ws[:, :, 1, :], lsrc[:, :, D + c * 128:D + (c + 1) * 128], reads=["s_lin"], writes=[ws[:, :, 1, :]], key=("wsl", c % 2, 1))
            DMA("sp", wx_[:, 0:4:2, :], s_wa.ap()[:, c].rearrange("d i j -> i d j"), reads=["s_wa"], writes=[wx_[:, 0:4:2, :]], key=("wax", c % 2, 0))
            DMA("sp", wx_[:, 1:4:2, :], s_wx.ap()[:, c].rearrange("d i j -> i d j"), reads=["s_wx"], writes=[wx_[:, 1:4:2, :]], key=("wax", c % 2, 1))
            MEMSET("dve", U1[:, 0:1], 0.0)
            MEMSET("dve", U1[:, 257:260], 0.0)
            MEMSET("dve", U1[:, 2308:2310], 0.0)
            for ti, (c0, T, col) in enumerate(TILES):
                bank = pb[ti % 2]
                for kc in range(NCH):
                    MM(bank[:, :T], ws[:, kc, 1, :], xfull[:, kc, c0:c0 + T], kc == 0, kc == NCH - 1)
                uo = CO + c0 if ti == 0 else LO + (c0 - CTX)
                CP("act", U1[:, uo:uo + T], bank[:, :T])
            for (uo, c0, T) in ((CO, 0, CTX), (LO, CTX, SEQ)):
                ACTV(U2[:, c0:c0 + T], U1[:, uo - 1:uo - 1 + T], AF.Identity, scale=cwt[:, c, 0:1], bias=cbt[:, c:c + 1])

        def front_b(c):
            U1, U2, ubf = UU[c % 2]
            for (uo, c0, T) in ((CO, 0, CTX), (LO, CTX, SEQ)):
                for tap in range(1, 4):
                    STT("dve", U2[:, c0:c0 + T], U1[:, uo - 1 + tap:uo - 1 + tap + T], cwt[:, c, tap:tap + 1], U2[:, c0:c0 + T], ALU.mult, ALU.add)

        def front_c(c):
            U1, U2, ubf = UU[c % 2]
            for (c0_, T_, _c) in TILES:
                CP("act", ubf[:, c0_:c0_ + T_], U2[:, c0_:c0_ + T_])

        def direction(c, dr):
            U1, U2, ubf = UU[c % 2]
            wx_ = wax[c % 2]
            T_ = (U1 if dr == 0 else U2)[:, 0:L]
            for ti, (c0, T, col) in enumerate(TILES):
                bank = nbank()
                MM(bank[:, :T], wx_[:, 2 * dr, :], ubf[:, c0:c0 + T], True, True)
                ACTV(LA[:, c0:c0 + T], bank[:, :T], AF.Sigmoid, bias=lpar[:, 0, dr, c:c + 1])
                bank2 = nbank()
                MM(bank2[:, :T], wx_[:, 2 * dr + 1, :], ubf[:, c0:c0 + T], True, True)
                ACTV(LBb[:, c0:c0 + T], bank2[:, :T], AF.Sigmoid, bias=lpar[:, 1, dr, c:c + 1])
            ACTV(LA, LA, AF.Exp, scale=lpar[:, 2, dr, c:c + 1])
            TT("dve", LBb, LBb, U2, ALU.mult)
            ACTV(T_, LA, AF.Square)
            ACTV(T_, T_, AF.Sqrt, scale=-1.0, bias=cvals[:, 1:2])
            TT("dve", LBb, LBb, T_, ALU.mult)
            so = T_
            if dr == 0:
                S.op("dve", lambda e: e.tensor_tensor_scan(out=so[:, 0:CTX], data0=LA[:, 0:CTX], data1=LBb[:, 0:CTX], initial=0.0, op0=ALU.mult, op1=ALU.add),
                     reads=[LA[:, 0:CTX], LBb[:, 0:CTX]], writes=[so[:, 0:CTX]])
                S.op("dve", lambda e: e.tensor_tensor_scan(out=so[:, CTX:L], data0=LA[:, CTX:L], data1=LBb[:, CTX:L], initial=so[:, CTX - 1:CTX], op0=ALU.mult, op1=ALU.add),
                     reads=[LA[:, CTX:L], LBb[:, CTX:L], so[:, CTX - 1:CTX]], writes=[so[:, CTX:L]])
            else:
                S.op("dve", lambda e: e.tensor_tensor_scan(out=so[:, CTX - 1::-1], data0=LA[:, CTX - 1::-1], data1=LBb[:, CTX - 1::-1], initial=0.0, op0=ALU.mult, op1=ALU.add),
                     reads=[LA[:, 0:CTX], LBb[:, 0:CTX]], writes=[so[:, 0:CTX]])
                S.op("dve", lambda e: e.tensor_tensor_scan(out=so[:, L - 1:CTX - 1:-1], data0=LA[:, L - 1:CTX - 1:-1], data1=LBb[:, L - 1:CTX - 1:-1], initial=so[:, 0:1], op0=ALU.mult, op1=ALU.add),
                     reads=[LA[:, CTX:L], LBb[:, CTX:L], so[:, 0:1]], writes=[so[:, CTX:L]])

        def tail(c):
            U1, U2, ubf = UU[c % 2]
            ws = wsl[c % 2]
            cg = c % 4
            TT("dve", LA[:, CTX:L], U1[:, CTX:L], U2[:, CTX:L], ALU.add)
            for ti in range(1, 5):
                c0, T, _ = TILES[ti]
                bank = pb[ti % 2]
                for kc in range(NCH):
                    MM(bank[:, :T], ws[:, kc, 0, :], xfull[:, kc, c0:c0 + T], kc == 0, kc == NCH - 1)
                ACTV(LBb[:, c0:c0 + T], bank[:, :T], AF.Gelu_apprx_tanh)
                TT("dve", gy[:, cg, c0 - CTX:c0 - CTX + T], LBb[:, c0:c0 + T], LA[:, c0:c0 + T], ALU.mult)

        def outproj(grp):
            DMA("sp", w_loh, s_lo.ap()[grp * 512:(grp + 1) * 512, :].rearrange("(c p) n -> p c n", p=128), reads=["s_lo"], writes=[w_loh])
            for ti in range(1, 5):
                c0, T, _ = TILES[ti]
                for ch in range(NCH):
                    yps = pb[ch % 2]
                    for c4 in range(4):
                        MM(yps[:, :T], w_loh[:, c4, ch * 128:(ch + 1) * 128], gy[:, c4, c0 - CTX:c0 - CTX + T], c4 == 0, c4 == 3)
                    if grp == 0:
                        hs = h[:, ch, c0:c0 + T]
                        STT("dve", hs, yps[:, :T], mvec(l, 5, ch, seq), hs, ALU.mult, ALU.add)
                    else:
                        resid(ch, yps[:, :T], l, 1, c0, T, seq)
                        drain(1)
                if grp == 1:
                    drain()
                    layernorm(l, 1, c0, T)

        front_a(0)
        front_b(0)
        front_c(0)
        for c in range(NCH):
            nxt = c + 1 < NCH
            if nxt:
                front_a(c + 1)
                front_b(c + 1)
            direction(c, 0)
            if nxt:
                front_c(c + 1)
            direction(c, 1)
            tail(c)
            if c % 4 == 3:
                outproj(c // 4)

    for seq in range(SPC):
        cnt["dma"] = 1000
        load_seq(seq)
        stage = 0

        def done():
            return stop_after is not None and stage >= stop_after
        if seq == 0:
            castq.append(("ab", cast_ab))
            cast_ffn(0, 1, "f01")
            cast_ffn(1, 0, "f10")
            castq.append(("lru", cast_lru))
            cast_ffn(1, 1, "f11")
        ffn(0, 0, seq, TILES); stage = 1
        if not done():
            flush_tag("ab")
            mixer_ab(seq); stage = 2
        if not done():
            flush_tag("f01")
            ffn(0, 1, seq, TILES); stage = 3
        if not done():
            flush_tag("f10")
            ffn(1, 0, seq, TILES); stage = 4
        if not done():
            flush_tag("lru")
            mixer_lru(seq); stage = 5
        if not done():
            flush_tag("f11")
            ffn(1, 1, seq, TILES[1:]); stage = 6
        drain()
        store_seq(seq)

    S.emit(final_dma_waits=out_dmas)
    st.close()
    return nc, S


_CACHE = {}


def _prep_inputs(inputs, ncores=NCORES):
    f = lambda a: np.ascontiguousarray(np.asarray(a, dtype=np.float32))
    x = f(inputs["x"]); c = f(inputs["c"]); ctx = f(inputs["ctx"]); c_ctx = f(inputs["c_ctx"])
    shared = {k: f(inputs[k]) for k in ("w_mod", "ffn_w_gate", "ffn_w_up", "ffn_w_down", "mix_ab_w_in", "attn_sink", "pool_w",
                                         "mix_ab_w_out", "lru_w_in", "lru_wa", "lru_wx", "lru_w_out")}
    b_mod = f(inputs["b_mod"])
    bm = b_mod.reshape(2, 72, 128).transpose(0, 2, 1)
    shared["b_mod3"] = np.ascontiguousarray(np.repeat(bm[:, :, :, None], 3, axis=3))
    pl = lambda a: np.ascontiguousarray(a)
    shared["ln_g"] = pl(f(inputs["ln_g"]).reshape(2, 3, NCH, 128).transpose(3, 0, 1, 2))
    shared["ln_b"] = pl(f(inputs["ln_b"]).reshape(2, 3, NCH, 128).transpose(3, 0, 1, 2))
    shared["pool_scale"] = pl(f(inputs["pool_scale"]).reshape(4, 128).T)
    shared["lru_conv_w"] = pl(f(inputs["lru_conv_w"]).reshape(4, NCH, 128).transpose(2, 1, 0))
    shared["lru_conv_b"] = pl(f(inputs["lru_conv_b"]).reshape(NCH, 128).T)
    for k in ("lru_ba", "lru_bx", "lru_lambda"):
        shared[k] = pl(f(inputs[k]).reshape(2, NCH, 128).transpose(2, 0, 1))
    shared.update(_host_consts())
    in_maps = []
    for i in range(ncores):
        m = dict(shared)
        m["x"] = np.ascontiguousarray(x[SPC * i:SPC * (i + 1)])
        m["ctx"] = np.ascontiguousarray(ctx[SPC * i:SPC * (i + 1)])
        cv = np.stack([c[SPC * i], c[SPC * i + 1], c_ctx], axis=-1)
        m["cvec"] = np.ascontiguousarray(cv.reshape(NCH, 128, 3).transpose(1, 0, 2))
        in_maps.append(m)
    return in_maps


def kernel(**inputs):
    if "nc" not in _CACHE:
        _CACHE["nc"] = build()[0]
    nc = _CACHE["nc"]
    in_maps = _prep_inputs(inputs)
    res = run_bass_kernel_spmd(nc, in_maps, core_ids=list(range(NCORES)))
    return np.concatenate([r["out"] for r in res.results], axis=0)
```

```python
import contextlib
import math
import numpy as np
import ml_dtypes
import concourse.bass as bass
import concourse.mybir as mybir
from concourse.bass_utils import run_bass_kernel_spmd

F32 = mybir.dt.float32
BF = mybir.dt.bfloat16
AF = mybir.ActivationFunctionType
ALU = mybir.AluOpType

ENGINES = ("pe", "act", "dve", "pool", "sp")
GRAN = 256
_DTSIZE = {}


def _dtsize(dt):
    s = _DTSIZE.get(dt)
    if s is None:
        name = str(dt)
        s = 4 if "32" in name else (2 if "16" in name else (8 if "64" in name else 1))
        _DTSIZE[dt] = s
    return s


def ap_granules(ap):
    t = ap.tensor
    name = t.name
    pat = ap.ap
    esz = _dtsize(ap.dtype)
    pstride = pat[0][0]
    off = ap.offset
    if pstride > 0:
        off = off % pstride
    dims = [(s, c) for (s, c) in pat[1:] if c > 1]
    if not dims:
        return name, {(off * esz) // GRAN}
    s_in, c_in = dims[-1]
    outer = dims[:-1]
    nouter = 1
    for _, c in outer:
        nouter *= c
    gr = set()
    if nouter > 256 or abs(s_in) > 1:
        lo = hi = off
        for s, c in dims:
            if s >= 0:
                hi += s * (c - 1)
            else:
                lo += s * (c - 1)
        gr.update(range((lo * esz) // GRAN, ((hi + 1) * esz - 1) // GRAN + 1))
        return name, gr
    starts = [off]
    for s, c in outer:
        starts = [b + s * i for b in starts for i in range(c)]
    for b in starts:
        if s_in >= 0:
            lo, hi = b, b + s_in * (c_in - 1)
        else:
            lo, hi = b + s_in * (c_in - 1), b
        gr.update(range((lo * esz) // GRAN, ((hi + 1) * esz - 1) // GRAN + 1))
    return name, gr


class Instr:
    __slots__ = ("eng", "fn", "deps", "signal", "value", "is_dma", "sem_key", "idx", "dma_value")

    def __init__(self, eng, fn, is_dma=False, sem_key=None):
        self.eng = eng
        self.fn = fn
        self.deps = set()
        self.signal = False
        self.value = None
        self.is_dma = is_dma
        self.sem_key = sem_key
        self.idx = None
        self.dma_value = None


class Sched:
    def __init__(self, nc):
        self.nc = nc
        self.streams = {e: [] for e in ENGINES}
        self.writer = {}
        self.readers = {}
        self.dma_counts = {}
        self.n_instr = 0

    def _add_dep(self, ins, prod, raw):
        if prod is None or prod is ins:
            return
        if prod.is_dma:
            ins.deps.add(prod)
            return
        if prod.eng == ins.eng and not ins.is_dma:
            if prod.eng == "pe":
                return
        prod.signal = True
        ins.deps.add(prod)

    def _keys(self, items, excl=None):
        out = []
        for it in items:
            if isinstance(it, (str, tuple)):
                out.append(it)
            else:
                name = it.tensor.name
                if name.startswith("pb"):
                    (excl if excl is not None else out).append((name, "bank"))
                    continue
                name, gr = ap_granules(it)
                out.extend((name, g) for g in gr)
        return out

    def op(self, eng, fn, reads=(), writes=(), sem_key=None):
        is_dma = sem_key is not None
        ins = Instr(eng, fn, is_dma=is_dma, sem_key=sem_key)
        ins.idx = len(self.streams[eng])
        wk = self._keys(writes)
        rk = self._keys(reads, excl=wk)
        W = self.writer
        RD = self.readers
        for b in rk:
            self._add_dep(ins, W.get(b), True)
        for b in wk:
            self._add_dep(ins, W.get(b), False)
            rd = RD.get(b)
            if rd:
                for r in rd.values():
                    self._add_dep(ins, r, False)
        for b in rk:
            rd = RD.get(b)
            if rd is None:
                rd = RD[b] = {}
            if is_dma:
                rd[("dma", id(ins))] = ins
            else:
                rd[eng] = ins
        for b in wk:
            W[b] = ins
            RD[b] = {}
        if is_dma:
            v = self.dma_counts.get(sem_key, 0) + 16
            self.dma_counts[sem_key] = v
            ins.dma_value = v
        self.streams[eng].append(ins)
        self.n_instr += 1
        if not is_dma:
            self.last = ins
        return ins

    def mark(self, key):
        self.writer[key] = self.last
        self.readers[key] = {}

    def emit(self, final_dma_waits=()):
        nc = self.nc
        for e in ENGINES:
            c = 0
            for ins in self.streams[e]:
                if ins.is_dma:
                    continue
                if ins.signal:
                    c += 1
                ins.value = c
        with contextlib.ExitStack() as st:
            esem = {e: st.enter_context(nc.semaphore("prog_" + e)) for e in ENGINES}
            dsem = {k: st.enter_context(nc.semaphore("dma_%d" % i)) for i, k in enumerate(self.dma_counts)}
            block = st.enter_context(nc.Block())

            def run_stream(e, eng):
                waited = {}
                for ins in self.streams[e]:
                    need = {}
                    for p in ins.deps:
                        if p.is_dma:
                            key = ("d", p.sem_key)
                            val = p.dma_value
                        else:
                            key = ("e", p.eng)
                            val = p.value
                        if val > need.get(key, 0):
                            need[key] = val
                    for key, val in need.items():
                        if waited.get(key, 0) >= val:
                            continue
                        waited[key] = val
                        sem = dsem[key[1]] if key[0] == "d" else esem[key[1]]
                        eng.wait_ge(sem, val)
                    r = ins.fn(eng)
                    if ins.is_dma:
                        r.then_inc(dsem[ins.sem_key], 16)
                    elif ins.signal:
                        r.then_inc(esem[e], 1)
                for p in final_dma_waits:
                    if p.eng == e:
                        eng.wait_ge(dsem[p.sem_key], p.dma_value)

            @block.sync
            def _(eng):
                run_stream("sp", eng)

            @block.scalar
            def _(eng):
                run_stream("act", eng)

            @block.vector
            def _(eng):
                run_stream("dve", eng)

            @block.gpsimd
            def _(eng):
                run_stream("pool", eng)

            @block.tensor
            def _(eng):
                run_stream("pe", eng)


D = 1024
NCH = 8
SEQ = 2048
CTX = 256
L = CTX + SEQ
DFF = 2816
NJ = 22
NB = 16
NCORES = 8
SPC = NB // NCORES
ALPHA = 4.0 ** 0.25
EPS_P = 1e-5 / (ALPHA * ALPHA)
NEG = -30000.0
ARENA_BYTES = 212736
import os as _os
_ABP = int(_os.environ.get("ABP", "4"))
_ABQ = int(_os.environ.get("ABQ", "9"))

TILES = [(0, CTX, 2)] + [(CTX + 512 * i, 512, None) for i in range(4)]


def _host_consts():
    rows = SEQ // 64
    t = np.arange(SEQ)
    row = (t // 64).astype(np.float32)
    col = (t % 64).astype(np.float32)
    inv = (10000.0 ** (-np.arange(16, dtype=np.float32) / 16)).astype(np.float32)
    ang = np.concatenate([row[:, None] * inv, col[:, None] * inv], axis=-1).astype(np.float32)
    cos = np.cos(ang).astype(np.float32)
    sin = np.sin(ang).astype(np.float32)
    cs = np.zeros((128, 2, SEQ), np.float32)
    for p in range(128):
        d = p % 64
        j = d % 32
        cs[p, 0] = cos[:, j]
        cs[p, 1] = (-sin[:, j]) if d < 32 else sin[:, j]
    i = np.arange(128)[:, None]
    j = np.arange(128)[None, :]
    m_prev = np.where(j <= i, 0.0, NEG).astype(np.float32)
    m_next = np.where(i <= j, 0.0, NEG).astype(np.float32)
    masks = np.stack([np.tile(m_prev, (1, 4)), np.tile(m_next, (1, 4))], axis=1)
    pm = np.zeros((128, 4, 5, 128), np.float32)
    for g, w in enumerate((2, 4, 8, 16)):
        r = w // 2
        n = 3 * 128
        full_mid = np.zeros((n, n), np.float64)
        for tt in range(n):
            lo, hi = max(tt - r, 0), min(tt + r, n - 1)
            full_mid[lo:hi + 1, tt] = 1.0 / (hi - lo + 1)
            full_mid[tt, tt] -= 1.0
        pm[:, g, 0] = full_mid[0:128, 128:256]
        pm[:, g, 2] = full_mid[128:256, 128:256]
        pm[:, g, 4] = full_mid[256:384, 128:256]
        pm[:, g, 1] = full_mid[0:128, 0:128]
        pm[:, g, 3] = full_mid[256:384, 256:384]
    ident = np.eye(128, dtype=np.float32)
    bf = ml_dtypes.bfloat16
    return {
        "k_cs": cs,
        "k_masks": masks.astype(bf),
        "k_pm": pm.astype(bf),
        "k_ident": ident,
        "k_identb": ident.astype(bf),
    }


def build(stop_after=None):
    nc = bass.Bass("TRN2", target_bir_lowering=False)

    def din(name, shape, dt=F32):
        return nc.dram_tensor(name, list(shape), dt, kind="ExternalInput")

    def dscr(name, shape, dt=BF):
        return nc.dram_tensor(name, list(shape), dt, kind="Internal")

    x_d = din("x", [SPC, SEQ, D])
    ctx_d = din("ctx", [SPC, CTX, D])
    cv_d = din("cvec", [128, NCH, 3])
    wmod_d = din("w_mod", [2, D, 9 * D])
    bmod_d = din("b_mod3", [2, 128, 72, 3])
    lng_d = din("ln_g", [128, 2, 3, NCH])
    lnb_d = din("ln_b", [128, 2, 3, NCH])
    wg_d = din("ffn_w_gate", [2, 2, D, DFF])
    wu_d = din("ffn_w_up", [2, 2, D, DFF])
    wd_d = din("ffn_w_down", [2, 2, DFF, D])
    win_d = din("mix_ab_w_in", [1, D, 1280])
    sink_d = din("attn_sink", [1, 8])
    pw_d = din("pool_w", [1, 4, 128, 128])
    psc_d = din("pool_scale", [128, 4])
    wout_d = din("mix_ab_w_out", [1, D, D])
    lin_d = din("lru_w_in", [1, D, 2 * D])
    cw_d = din("lru_conv_w", [128, NCH, 4])
    cb_d = din("lru_conv_b", [128, NCH])
    wa_d = din("lru_wa", [1, 2, 8, 128, 128])
    ba_d = din("lru_ba", [128, 2, NCH])
    wx_d = din("lru_wx", [1, 2, 8, 128, 128])
    bx_d = din("lru_bx", [128, 2, NCH])
    lam_d = din("lru_lambda", [128, 2, NCH])
    lout_d = din("lru_w_out", [1, D, D])
    kcs_d = din("k_cs", [128, 2, SEQ])
    kmask_d = din("k_masks", [128, 2, 512], BF)
    kpm_d = din("k_pm", [128, 4, 5, 128], BF)
    kid_d = din("k_ident", [128, 128])
    kidb_d = din("k_identb", [128, 128], BF)
    out_d = nc.dram_tensor("out", [SPC, SEQ, D], F32, kind="ExternalOutput")

    s_g = [[dscr("s_g%d%d" % (l, f), [D, DFF]) for f in range(2)] for l in range(2)]
    s_u = [[dscr("s_u%d%d" % (l, f), [D, DFF]) for f in range(2)] for l in range(2)]
    s_d = [[dscr("s_d%d%d" % (l, f), [DFF, D]) for f in range(2)] for l in range(2)]
    s_qk = dscr("s_qk", [D, 1280])
    s_qks = dscr("s_qks", [D, 640])
    s_pw = dscr("s_pw", [4, 128, 128])
    s_wo = dscr("s_wo", [D, D])
    s_lin = dscr("s_lin", [D, 2 * D])
    s_wa = dscr("s_wa", [2, 8, 128, 128])
    s_wx = dscr("s_wx", [2, 8, 128, 128])
    s_lo = dscr("s_lo", [D, D])

    st = contextlib.ExitStack()
    arena = st.enter_context(nc.sbuf_tensor("arena", [128, ARENA_BYTES // 2], BF))
    pb = [st.enter_context(nc.psum_tensor("pb%d" % i, [128, 512], F32)) for i in range(8)]
    S = Sched(nc)

    def V(off, shape, dt):
        n = int(np.prod(shape))
        esz = 4 if dt == F32 else 2
        assert off % 256 == 0 or True
        assert off + n * esz <= ARENA_BYTES, (off, shape)
        v = arena[:, off // 2:(off + n * esz) // 2]
        if dt == F32:
            v = v.bitcast(F32)
        if len(shape) == 2:
            v = v.rearrange("p (a b) -> p a b", a=shape[0])
        elif len(shape) == 3:
            v = v.rearrange("p (a b c) -> p a b c", a=shape[0], b=shape[1])
        return v

    o = 0
    h = V(o, [NCH, L], F32); o += NCH * L * 4
    ident = V(o, [128], F32); o += 512
    identb = V(o, [128], BF); o += 256
    onesN = V(o, [128], BF); o += 256
    onesA = V(o, [128], BF); o += 256
    onesB = V(o, [128], BF); o += 256
    masks = V(o, [2, 512], BF); o += 2048
    modv = [V(o + l * 1024, [72, 3], F32) for l in range(2)]; o += 2048
    lng = V(o, [2, 3, NCH], F32); o += 256
    lnb = V(o, [2, 3, NCH], F32); o += 256
    cvals = V(o, [4], F32); o += 256
    cvt = V(o, [NCH, 3], F32); o += 256
    expsink = V(o, [8], F32); o += 256
    psc = V(o, [4], F32); o += 256
    RB = o
    assert RB % 256 == 0

    def RV(off, shape, dt):
        return V(RB + off, shape, dt)

    r_bf = RV(0, [NCH, 512], BF)
    rsq_bf = RV(8192, [NCH, 512], BF)
    mean_sb = RV(16384, [512], F32)
    var_sb = RV(18432, [512], F32)
    rstd_sb = RV(20480, [512], F32)
    tn = [RV(22528 + 2048 * i, [512], F32) for i in range(2)]
    sg = [RV(26624 + 2048 * i, [512], F32) for i in range(2)]
    xmod = [RV(30720 + 8192 * i, [NCH, 512], BF) for i in range(2)]
    PH = 47104

    cnt = {"dma": 0}

    def MM(out, lhsT, rhs, start, stop):
        S.op("pe", lambda e: e.matmul(out, lhsT=lhsT, rhs=rhs, start=start, stop=stop), reads=[lhsT, rhs], writes=[out])

    def TR(out, in_, idn):
        S.op("pe", lambda e: e.transpose(out, in_, idn), reads=[in_, idn], writes=[out])

    def ACTV(out, in_, func, scale=1.0, bias=None, eng="act"):
        rd = [in_]
        kw = {}
        if not isinstance(scale, (int, float)):
            rd.append(scale)
        if bias is not None:
            if not isinstance(bias, (int, float)):
                rd.append(bias)
            kw["bias"] = bias
        S.op("act", lambda e: e.activation(out=out, in_=in_, func=func, scale=scale, **kw), reads=rd, writes=[out])

    def TT(eng, out, in0, in1, op):
        S.op(eng, lambda e: e.tensor_tensor(out=out, in0=in0, in1=in1, op=op), reads=[in0, in1], writes=[out])

    def TS(eng, out, in0, s1, op0, s2=None, op1=None):
        rd = [in0] + [s for s in (s1, s2) if s is not None and not isinstance(s, (int, float))]
        if op1 is None:
            S.op(eng, lambda e: e.tensor_scalar(out=out, in0=in0, scalar1=s1, scalar2=None, op0=op0), reads=rd, writes=[out])
        else:
            S.op(eng, lambda e: e.tensor_scalar(out=out, in0=in0, scalar1=s1, scalar2=s2, op0=op0, op1=op1), reads=rd, writes=[out])

    def STT(eng, out, in0, scalar, in1, op0, op1):
        rd = [in0, in1] + ([] if isinstance(scalar, (int, float)) else [scalar])
        S.op(eng, lambda e: e.scalar_tensor_tensor(out=out, in0=in0, scalar=scalar, in1=in1, op0=op0, op1=op1), reads=rd, writes=[out])

    def CP(eng, out, in_):
        if eng == "act":
            S.op("act", lambda e: e.activation(out=out, in_=in_, func=AF.Copy), reads=[in_], writes=[out])
        else:
            S.op(eng, lambda e: e.tensor_copy(out=out, in_=in_), reads=[in_], writes=[out])

    def MEMSET(eng, out, val):
        S.op(eng, lambda e: e.memset(out, val), writes=[out])

    def DMA(eng, out, in_, reads=(), writes=(), key=None, **kw):
        if key is None:
            cnt["dma"] += 1
            key = ("u", cnt["dma"])
        return S.op(eng, lambda e: e.dma_start(out=out, in_=in_, **kw), reads=list(reads), writes=list(writes), sem_key=key)

    def dap(t, offset, pat):
        return bass.AP(t, offset, [list(p) for p in pat])

    gate = {"k": None, "n": 0}

    def new_gate():
        gate["n"] += 1
        gate["k"] = ("gate", gate["n"])
        S.mark(gate["k"])

    def grd():
        return [gate["k"]] if gate["k"] is not None else []

    def cast2d(dst_t, src_t, src_off, nelem, rowlen, key):
        nrow = nelem // rowlen
        DMA("pool", dap(dst_t, 0, [[rowlen, nrow], [1, rowlen]]), dap(src_t, src_off, [[rowlen, nrow], [1, rowlen]]), reads=grd(), writes=[key], key=("c", key))

    castq = []

    def cast_ffn(l, f, tag=None):
        off = (l * 2 + f) * D * DFF
        first = (l, f) == (0, 0)
        pcs = []
        for g in range(11):
            for (dst_t, src_t, nm) in ((s_g[l][f], wg_d, "s_g"), (s_u[l][f], wu_d, "s_u")):
                wk = [("s_g", l, f, g), ("s_u", l, f, g)] if first else [(nm, l, f)]
                sk = ("c", "gu", l, f, g) if first else ("c", nm, l, f)
                pcs.append(lambda dst_t=dst_t, src_t=src_t, wk=wk, sk=sk, g=g: DMA(
                    "pool", dap(dst_t, g * 256, [[DFF, D], [1, 256]]), dap(src_t, off + g * 256, [[DFF, D], [1, 256]]),
                    reads=grd(), writes=wk, key=sk))
        for dp in range(4):
            wk = ("s_d", l, f, dp) if first else ("s_d", l, f)
            sk = ("c", "s_d", l, f, dp) if first else ("c", "s_d", l, f)
            pcs.append(lambda wk=wk, sk=sk, dp=dp: DMA(
                "pool", dap(s_d[l][f], dp * 256, [[D, DFF], [1, 256]]), dap(wd_d, off + dp * 256, [[D, DFF], [1, 256]]),
                reads=grd(), writes=[wk], key=sk))
        if tag is None:
            for p_ in pcs:
                p_()
        else:
            castq.extend((tag, p_) for p_ in pcs)

    def tick(n=1):
        for _ in range(n):
            if castq:
                new_gate()
                castq.pop(0)[1]()

    def flush_tag(tag):
        while any(t == tag for t, _ in castq):
            new_gate()
            castq.pop(0)[1]()

    def cast_ab():
        for kvh in range(2):
            DMA("pool", dap(s_qk, kvh * 64, [[1280, D], [128, 4], [1, 64]]),
                dap(win_d, kvh * 256, [[1280, D], [64, 4], [1, 64]]), reads=grd(), writes=[("s_qk", "q", kvh)], key=("c", "qk", kvh))
        DMA("pool", dap(s_qk, 512, [[1280, D], [1, 768]]), dap(win_d, 512, [[1280, D], [1, 768]]), reads=grd(), writes=[("s_qk", "r")], key=("c", "qk", 2))
        for kvh in range(2):
            for half in range(2):
                DMA("pool", dap(s_qks, kvh * 64 + half * 32, [[640, D], [128, 4], [1, 32]]),
                    dap(win_d, kvh * 256 + (1 - half) * 32, [[1280, D], [64, 4], [1, 32]]),
                    reads=grd(), writes=[("s_qks", kvh, half)], key=("c", "qks", kvh, half))
        for half in range(2):
            DMA("pool", dap(s_qks, 512 + half * 32, [[640, D], [64, 2], [1, 32]]),
                dap(win_d, 512 + (1 - half) * 32, [[1280, D], [64, 2], [1, 32]]),
                reads=grd(), writes=[("s_qks", "k", half)], key=("c", "qks", "k", half))
        cast2d(s_pw, pw_d, 0, 4 * 128 * 128, 2048, "s_pw")
        cast2d(s_wo, wout_d, 0, D * D, 1024, "s_wo")

    def cast_lru():
        cast2d(s_lin, lin_d, 0, D * 2 * D, 2048, "s_lin")
        cast2d(s_wa, wa_d, 0, 2 * 8 * 128 * 128, 2048, "s_wa")
        cast2d(s_wx, wx_d, 0, 2 * 8 * 128 * 128, 2048, "s_wx")
        cast2d(s_lo, lout_d, 0, D * D, 1024, "s_lo")

    AB_KEYS = [("s_qk", "q", 0), ("s_qk", "q", 1), ("s_qk", "r")] + [("s_qks", a, b) for a in (0, 1, "k") for b in (0, 1)]

    DMA("sp", ident, kid_d.ap(), writes=[ident])
    DMA("sp", identb, kidb_d.ap(), writes=[identb])
    DMA("sp", masks, kmask_d.ap(), writes=[masks])
    DMA("sp", lng, lng_d.ap(), writes=[lng])
    DMA("sp", lnb, lnb_d.ap(), writes=[lnb])
    DMA("sp", cvt, cv_d.ap(), writes=[cvt])
    DMA("sp", expsink, sink_d.ap()[0].partition_broadcast(128), writes=[expsink])
    MEMSET("dve", onesN, 1.0 / 1024.0)
    MEMSET("dve", onesA[:, 0:64], 1.0)
    MEMSET("dve", onesA[:, 64:128], 0.0)
    MEMSET("dve", onesB[:, 0:64], 0.0)
    MEMSET("dve", onesB[:, 64:128], 1.0)
    MEMSET("dve", cvals[:, 0:1], EPS_P)
    MEMSET("dve", cvals[:, 1:2], 1.0)
    MEMSET("dve", cvals[:, 2:3], 0.0)
    cast_ffn(0, 0)

    ACTV(cvt, cvt, AF.Silu)
    bm = RV(PH + 2 * 8192, [72, 3], F32)
    def mod_finalize(l, mps, bm_):
        TT("dve", modv[l], mps, bm_, ALU.add)
        for i in range(3):
            sc = modv[l][:, (3 * i + 1) * 8:(3 * i + 2) * 8, :]
            TS("dve", sc, sc, 1.0, ALU.add)
            gt = modv[l][:, (3 * i + 2) * 8:(3 * i + 3) * 8, :]
            TS("dve", gt, gt, (1.0 if i == 1 else 0.5) / ALPHA, ALU.mult)

    def mod_block(l, b, slot, msb, bank, skey):
        DMA("sp", slot, wmod_d.ap()[l].rearrange("(kc p) n -> p kc n", p=128)[:, :, b * 256:(b + 1) * 256], writes=[slot], key=skey)
        for kc in range(NCH):
            MM(bank[0:3, 256:512], cvt[:, kc, :], slot[:, kc, :], kc == 0, kc == NCH - 1)
        CP("dve", msb[0:3, 0:256], bank[0:3, 256:512])
        for q in range(2):
            fc = 2 * b + q
            TR(bank[:, fc * 3:fc * 3 + 3], msb[0:3, q * 128:(q + 1) * 128], ident[0:3, 0:3])

    wm0 = [RV(PH + i * 8192, [NCH, 256], F32) for i in range(2)]
    DMA("sp", bm, bmod_d.ap()[0], writes=[bm])
    for b in range(36):
        mod_block(0, b, wm0[b % 2], sg[0], pb[3], ("wm", b % 2))
    mod_finalize(0, pb[3][:, 0:216].rearrange("p (a b) -> p a b", b=3), bm)

    def mod1_step(k, SBo):
        slot = RV(SBo + 10240, [NCH, 256], F32)
        bm1 = sg[1][:, 0:216].rearrange("p (a b) -> p a b", b=3)
        if k == 0:
            DMA("sp", bm1, bmod_d.ap()[1], writes=[bm1])
        for b in (2 * k, 2 * k + 1):
            mod_block(1, b, slot, sg[0], pb[3], ("wm1", 0))
        if k == 17:
            mod_finalize(1, pb[3][:, 0:216].rearrange("p (a b) -> p a b", b=3), bm1)

    def mvec(l, v, ch, col):
        return modv[l][:, v * 8 + ch, col:col + 1]

    ACTV(expsink, expsink, AF.Exp)

    def modulate(dst, l, i, c0, T, col, dcol0=0):
        for ch in range(NCH):
            ACTV(dst[:, ch, dcol0:dcol0 + T], h[:, ch, c0:c0 + T], AF.Identity, scale=mvec(l, 3 * i + 1, ch, col), bias=mvec(l, 3 * i, ch, col))

    def resid(ch, yps, l, i, c0, T, col):
        hs = h[:, ch, c0:c0 + T]
        STT("dve", hs, yps, mvec(l, 3 * i + 2, ch, col), hs, ALU.mult, ALU.add)
        CP("act", r_bf[:, ch, :T], hs)
        ACTV(rsq_bf[:, ch, :T], hs, AF.Square)

    def layernorm(l, i, c0, T):
        for ch in range(NCH):
            MM(pb[6][:, :T], onesN, r_bf[:, ch, :T], ch == 0, ch == NCH - 1)
        for ch in range(NCH):
            MM(pb[7][:, :T], onesN, rsq_bf[:, ch, :T], ch == 0, ch == NCH - 1)
        CP("act", mean_sb[:, :T], pb[6][:, :T])
        TT("dve", var_sb[:, :T], mean_sb[:, :T], mean_sb[:, :T], ALU.mult)
        TT("dve", var_sb[:, :T], pb[7][:, :T], var_sb[:, :T], ALU.subtract)
        ACTV(rstd_sb[:, :T], var_sb[:, :T], AF.Sqrt, bias=cvals[:, 0:1])
        S.op("dve", lambda e: e.reciprocal(out=rstd_sb[:, :T], in_=rstd_sb[:, :T]), reads=[rstd_sb[:, :T]], writes=[rstd_sb[:, :T]])
        def mk(ch):
            def _f():
                t_ = tn[ch % 2]
                hs = h[:, ch, c0:c0 + T]
                TT("dve", t_[:, :T], hs, mean_sb[:, :T], ALU.subtract)
                TT("dve", t_[:, :T], t_[:, :T], rstd_sb[:, :T], ALU.mult)
                ACTV(hs, t_[:, :T], AF.Identity, scale=lng[:, l, i, ch:ch + 1], bias=lnb[:, l, i, ch:ch + 1])
            return _f
        for ch in range(NCH):
            pending.append(mk(ch))

    pending = []

    def drain(n=None):
        k = len(pending) if n is None else min(n, len(pending))
        for _ in range(k):
            pending.pop(0)()

    hid = RV(PH, [NJ, 512], BF)
    wA = [[RV(PH + 22528 + s * 8192 + t * 4096, [NCH, 256], BF) for t in range(2)] for s in range(3)]
    wB = [RV(PH + 22528 + 24576 + s * 11264, [NJ, 256], BF) for s in range(2)]

    def ffn(l, f, seq, tiles, hooks=None):
        i = 0 if f == 0 else 2
        gsrc = s_g[l][f].ap().rearrange("(kc p) n -> p kc n", p=128)
        usrc = s_u[l][f].ap().rearrange("(kc p) n -> p kc n", p=128)
        dsrc = s_d[l][f].ap().rearrange("(j p) n -> p j n", p=128)
        steps = []
        for ti in range(len(tiles)):
            for g in range(11):
                steps.append(("A", ti, g))
            for dp in range(4):
                steps.append(("B", ti, dp))
        ia = {"A": 0, "B": 0}
        jcount = 0
        for k in range(len(steps)):
            kind, ti, g = steps[k]
            if seq == 0 and k % 3 == 2:
                tick()
            if kind == "A":
                s = ia["A"] % 3
                ia["A"] += 1
                DMA("sp", wA[s][0], gsrc[:, :, g * 256:(g + 1) * 256], reads=[("s_g", l, f, g) if (l, f) == (0, 0) else ("s_g", l, f)], writes=[wA[s][0]], key=("wA", s, 0))
                DMA("sp", wA[s][1], usrc[:, :, g * 256:(g + 1) * 256], reads=[("s_u", l, f, g) if (l, f) == (0, 0) else ("s_u", l, f)], writes=[wA[s][1]], key=("wA", s, 1))
            else:
                s = ia["B"] % 2
                ia["B"] += 1
                DMA("sp", wB[s], dsrc[:, :, g * 256:(g + 1) * 256], reads=[("s_d", l, f, g) if (l, f) == (0, 0) else ("s_d", l, f)], writes=[wB[s]], key=("wB", s))
            kind, ti, g = steps[k]
            c0, T, col = tiles[ti]
            if col is None:
                col = seq
            xm = xmod[ti % 2]
            if kind == "A":
                if g == 0 and ti == 0:
                    modulate(xm, l, i, c0, T, col)
                for jj in range(2):
                    j = 2 * g + jj
                    ga = pb[jcount % 2]
                    ub = pb[2 + jcount % 2]
                    sgt = sg[jcount % 2]
                    jcount += 1
                    for kc in range(NCH):
                        MM(ga[:, :T], wA[s][0][:, kc, jj * 128:(jj + 1) * 128], xm[:, kc, :T], kc == 0, kc == NCH - 1)
                    for kc in range(NCH):
                        MM(ub[:, :T], wA[s][1][:, kc, jj * 128:(jj + 1) * 128], xm[:, kc, :T], kc == 0, kc == NCH - 1)
                    ACTV(sgt[:, :T], ga[:, :T], AF.Silu)
                    TT("dve", hid[:, j, :T], sgt[:, :T], ub[:, :T], ALU.mult)
                    drain(1)
            else:
                if g == 0 and ti + 1 < len(tiles):
                    c0n, Tn, coln = tiles[ti + 1]
                    drain()
                    modulate(xmod[(ti + 1) % 2], l, i, c0n, Tn, seq if coln is None else coln)
                for dd in range(2):
                    ch = 2 * g + dd
                    yps = pb[4 + dd]
                    for j in range(NJ):
                        MM(yps[:, :T], wB[s][:, j, dd * 128:(dd + 1) * 128], hid[:, j, :T], j == 0, j == NJ - 1)
                    resid(ch, yps[:, :T], l, i, c0, T, col)
                if g == 3:
                    drain()
                    layernorm(l, 0 if f == 0 else 2, c0, T)
                    if hooks and ti in hooks:
                        hooks[ti]()

    stg = [RV(PH + i * 4096, [D], F32) for i in range(2)]

    def load_seq(seq):
        for blk in range(L // 128):
            sl = stg[blk % 2]
            if blk < 2:
                src = ctx_d.ap()[seq, blk * 128:(blk + 1) * 128, :]
            else:
                src = x_d.ap()[seq, (blk - 2) * 128:(blk - 1) * 128, :]
            DMA("sp", sl, src, writes=[sl], key=("stg", blk % 2))
            for half in range(2):
                bank = pb[(blk * 2 + half) % 4]
                for q in range(4):
                    ch = half * 4 + q
                    TR(bank[:, q * 128:(q + 1) * 128], sl[:, ch * 128:(ch + 1) * 128], ident)
                dst = h[:, half * 4:half * 4 + 4, blk * 128:(blk + 1) * 128]
                srcp = bank[:].rearrange("p (a b) -> p a b", b=128)
                CP("act" if half == 0 else "dve", dst, srcp)

    out_dmas = []

    def store_seq(seq):
        for blk in range(SEQ // 128):
            sl = stg[blk % 2]
            c0 = CTX + blk * 128
            for half in range(2):
                bank = pb[(blk * 2 + half) % 4]
                for q in range(4):
                    ch = half * 4 + q
                    TR(bank[:, q * 128:(q + 1) * 128], h[:, ch, c0:c0 + 128], ident)
                CP("act" if half == 0 else "dve", sl[:, half * 512:(half + 1) * 512], bank[:])
            out_dmas.append(DMA("sp", out_d.ap()[seq, blk * 128:(blk + 1) * 128, :], sl, reads=[sl], key=("ost", blk % 2)))

    q_rot = RV(PH, [4, L], BF)
    kTk = [RV(PH + 18432 + i * 4608, [L], BF) for i in range(2)]
    vtk = [RV(PH + 27648 + i * 4608, [L // 128, 128], BF) for i in range(2)]
    poolT = RV(PH + 36864, [4, L], BF)
    SB = PH + 55296
    w_qkv = RV(SB, [NCH, 1408], BF)
    csb = [RV(i * 4096, [2, 512], F32) for i in range(2)]
    rtmp = [RV(8192 + i * 2048, [512], F32) for i in range(2)]
    w_u = RV(SB, [NCH, 512], BF)
    utok = RV(SB + 8192, [8, 512], BF)
    w_pool = RV(SB + 16384, [4, 128], BF)
    pmt = RV(SB + 17408, [4, 5, 128], BF)
    dTt = [RV(SB + 22528 + i * 1024, [4, 128], BF) for i in range(2)]
    pT = [RV(SB + i * 1024, [512], BF) for i in range(4)]
    den = [RV(SB + 4096 + i * 2048, [512], F32) for i in range(2)]
    sinkrow = RV(SB + 8192, [512], F32)
    w_o = RV(SB, [8, D], BF)

    def mixer_ab(seq):
        l = 0
        src = s_qk.ap().rearrange("(kc p) n -> p kc n", p=128)
        srcs = s_qks.ap().rearrange("(kc p) n -> p kc n", p=128)
        DMA("sp", w_qkv[:, :, 0:768], src[:, :, 0:768], reads=AB_KEYS, writes=[w_qkv[:, :, 0:768]])
        DMA("sp", w_qkv[:, :, 768:1408], srcs, reads=AB_KEYS, writes=[w_qkv[:, :, 768:1408]])
        MEMSET("dve", kTk[0][64:128, :], 0.0)
        MEMSET("dve", kTk[1][0:64, :], 0.0)
        MEMSET("dve", vtk[0][:, :, 64:128], 0.0)
        MEMSET("dve", vtk[1][:, :, 0:64], 0.0)
        for ti, (c0, T, col) in enumerate(TILES):
            if col is None:
                col = seq
            xm = xmod[ti % 2]
            lat = ti > 0
            if lat:
                cs_ = csb[ti % 2]
                DMA("sp", cs_, kcs_d.ap()[:, :, (ti - 1) * 512:ti * 512], writes=[cs_], key=("cs", ti % 2))
                drain()
            modulate(xm, l, 1, c0, T, col)
            for c in range(5):
                drain(2)
                pq = pb[(c % 2) * 2]
                pqs = pb[(c % 2) * 2 + 1]
                wc = c * 128
                dst = q_rot[:, c, c0:c0 + T] if c < 4 else None
                for kc in range(NCH):
                    MM(pq[:, :T], w_qkv[:, kc, wc:wc + 128], xm[:, kc, :T], kc == 0, kc == NCH - 1)
                if lat:
                    for kc in range(NCH):
                        MM(pqs[:, :T], w_qkv[:, kc, 768 + wc:768 + wc + 128], xm[:, kc, :T], kc == 0, kc == NCH - 1)
                    TT("dve", rtmp[0][:, :T], pq[:, :T], cs_[:, 0, :T], ALU.mult)
                    TT("dve", rtmp[1][:, :T], pqs[:, :T], cs_[:, 1, :T], ALU.mult)
                    if c < 4:
                        TT("dve", dst, rtmp[0][:, :T], rtmp[1][:, :T], ALU.add)
                    else:
                        for hv in range(2):
                            pr_ = slice(hv * 64, hv * 64 + 64)
                            TT("dve", kTk[hv][pr_, c0:c0 + T], rtmp[0][pr_, :T], rtmp[1][pr_, :T], ALU.add)
                else:
                    if c < 4:
                        CP("act", dst, pq[:, :T])
                    else:
                        for hv in range(2):
                            pr_ = slice(hv * 64, hv * 64 + 64)
                            CP("act", kTk[hv][pr_, c0:c0 + T], pq[pr_, :T])
            nblk = T // 128
            for b in range(nblk):
                for kc in range(NCH):
                    MM(pb[4][:, b * 128:(b + 1) * 128], xm[:, kc, b * 128:(b + 1) * 128], w_qkv[:, kc, 640:768], kc == 0, kc == NCH - 1)
            vps = pb[4][:, :T].rearrange("p (a b) -> p a b", b=128)
            CP("act", vtk[0][:, c0 // 128:c0 // 128 + nblk, 0:64], vps[:, :, 0:64])
            CP("dve", vtk[1][:, c0 // 128:c0 // 128 + nblk, 64:128], vps[:, :, 64:128])
        if _ABP < 2:
            return
        DMA("sp", w_u, src[:, :, 768:1280], reads=AB_KEYS, writes=[w_u])
        DMA("sp", w_pool, s_pw.ap().rearrange("g c e -> c g e"), reads=["s_pw"], writes=[w_pool])
        DMA("sp", pmt, kpm_d.ap(), writes=[pmt])
        DMA("sp", psc, psc_d.ap(), writes=[psc])

        def ring(cb):
            return (cb + 6) % 8 if cb < 2 else (cb - 2) % 8

        def pool_block(cb):
            if _ABQ < 2:
                return
            first = cb in (0, 2)
            last = cb in (1, 17)
            bank = pb[2 + cb % 2]
            for g in range(4):
                terms = []
                if not first:
                    terms.append((cb - 1, 0))
                terms.append((cb, 1 if first else (3 if last else 2)))
                if not last:
                    terms.append((cb + 1, 4))
                for n_, (sb_, var) in enumerate(terms):
                    MM(bank[:, g * 128:(g + 1) * 128], utok[:, ring(sb_), g * 128:(g + 1) * 128], pmt[:, g, var, :], n_ == 0, n_ == len(terms) - 1)
            dt_ = dTt[cb % 2]
            CP("dve", dt_, bank[:].rearrange("p (a b) -> p a b", b=128))
            if _ABQ < 3:
                return
            bank2 = pb[4 + cb % 2]
            for g in range(4):
                MM(bank2[:, g * 128:(g + 1) * 128], w_pool[:, g, :], dt_[:, g, :], True, True)
            for g in range(4):
                ACTV(poolT[:, g, cb * 128:(cb + 1) * 128], bank2[:, g * 128:(g + 1) * 128], AF.Identity, scale=psc[:, g:g + 1])

        for ti, (c0, T, col) in enumerate(TILES):
            if col is None:
                col = seq
            xm = xmod[ti % 2]
            if _ABQ < 1:
                continue
            modulate(xm, l, 1, c0, T, col)
            nblk = T // 128
            for b in range(nblk):
                cb = c0 // 128 + b
                bank = pb[cb % 2]
                for kc in range(NCH):
                    MM(bank[:], xm[:, kc, b * 128:(b + 1) * 128], w_u[:, kc, :], kc == 0, kc == NCH - 1)
                CP("act", utok[:, ring(cb), :], bank[:])
            if ti == 0:
                pool_block(0)
                pool_block(1)
            else:
                b0 = c0 // 128
                if ti > 1:
                    pool_block(b0 - 1)
                for b in range(3):
                    pool_block(b0 + b)
                if ti == 4:
                    pool_block(b0 + 3)
        if _ABP < 3:
            return
        for kvh in range(2):
            for g in range(4):
                hd = kvh * 4 + g
                sl = sinkrow[kvh * 64:(kvh + 1) * 64, g * 128:(g + 1) * 128]
                ACTV(sl, ident[kvh * 64:(kvh + 1) * 64, :], AF.Identity, scale=0.0, bias=expsink[kvh * 64:(kvh + 1) * 64, hd:hd + 1])
        items = []
        for qb in range(L // 128):
            if qb < 2:
                keys = [(0, None), (1, None)]
            else:
                keys = []
                if qb - 1 >= 2:
                    keys.append((qb - 1, 0))
                keys.append((qb, None))
                if qb + 1 <= 17:
                    keys.append((qb + 1, 1))
                keys += [(0, None), (1, None)]
            for kvh in range(2):
                for n_, (kb, mk) in enumerate(keys):
                    items.append((qb, kvh, kb, mk, kvh == 0 and n_ == 0, kvh == 1 and n_ == len(keys) - 1))
        NI = len(items)
        LA = 2
        for t in range(NI + LA):
            if t < NI:
                qb, kvh, kb, mk, fst, lst = items[t]
                qc = qb * 128
                sps = pb[t % 3]
                ptt = pT[t % 4]
                MM(sps[:], kTk[kvh][:, kb * 128:(kb + 1) * 128], q_rot[:, :, qc:qc + 128], True, mk is None)
                if mk is not None:
                    MM(sps[:], identb, masks[:, mk, :], False, True)
                ACTV(ptt, sps[:], AF.Exp, scale=0.125)
            u_ = t - LA
            if u_ >= 0:
                qb, kvh, kb, mk, fst, lst = items[u_]
                qc = qb * 128
                ptt = pT[u_ % 4]
                ops_ = pb[4 + qb % 2]
                dps_ = pb[6 + qb % 2]
                MM(ops_[:], vtk[kvh][:, kb, :], ptt, fst, lst)
                MM(dps_[:], (onesA if kvh == 0 else onesB), ptt, fst, lst)
                if lst:
                    if seq == 0:
                        mod1_step(qb, SB)
                        tick()
                    dn = den[qb % 2]
                    TT("dve", dn, dps_[:], sinkrow, ALU.add)
                    S.op("dve", lambda e, dn=dn: e.reciprocal(out=dn, in_=dn), reads=[dn], writes=[dn])
                    TT("dve", q_rot[:, :, qc:qc + 128], ops_[:].rearrange("p (a b) -> p a b", b=128), dn.rearrange("p (a b) -> p a b", b=128), ALU.mult)
        if _ABP < 4:
            return
        wo_s = s_wo.ap()
        for two in range(2):
            DMA("sp", w_o[two * 64:(two + 1) * 64, 0:4, :], dap(s_wo, two * 256 * D, [[D, 64], [64 * D, 4], [1, D]]), reads=["s_wo"], writes=[w_o[:, 0:4, :]])
        DMA("sp", w_o[:, 4:8, :], wo_s[512:1024, :].rearrange("(g p) n -> p g n", p=128), reads=["s_wo"], writes=[w_o[:, 4:8, :]])
        for ti, (c0, T, col) in enumerate(TILES):
            if col is None:
                col = seq
            for ch in range(NCH):
                yps = pb[ch % 2]
                for g in range(4):
                    MM(yps[:, :T], w_o[:, g, ch * 128:(ch + 1) * 128], q_rot[:, g, c0:c0 + T], g == 0, False)
                for g in range(4):
                    MM(yps[:, :T], w_o[:, 4 + g, ch * 128:(ch + 1) * 128], poolT[:, g, c0:c0 + T], False, g == 3)
                resid(ch, yps[:, :T], l, 1, c0, T, col)
                drain(1)
            drain()
            layernorm(l, 1, c0, T)

    LA = RV(0, [L], F32)
    LBb = RV(9216, [L], F32)
    wax = [RV(23040 + i * 1024, [4, 128], BF) for i in range(2)]
    lpar = RV(43776, [5, 2, NCH], F32)
    cwt = RV(43776 + 512, [NCH, 4], F32)
    cbt = RV(43776 + 768, [NCH], F32)
    xfull = RV(PH, [NCH, L], BF)
    gy = RV(PH + 36864, [4, SEQ], BF)
    UB = PH + 36864 + 16384
    UU = [(RV(25088, [2368], F32), RV(34560, [L], F32), RV(18432, [L], BF)),
          (RV(UB, [2368], F32), RV(UB + 9472, [L], F32), RV(UB + 18688, [L], BF))]
    w_loh = RV(UB + 9472, [4, D], BF)
    wsl = [RV(UB + 23296 + i * 4096, [NCH, 2, 128], BF) for i in range(2)]
    CO = 1
    LO = 260

    def mixer_lru(seq):
        l = 1
        DMA("sp", lpar[:, 0], ba_d.ap(), writes=[lpar[:, 0]])
        DMA("sp", lpar[:, 1], bx_d.ap(), writes=[lpar[:, 1]])
        DMA("sp", lpar[:, 2], lam_d.ap(), writes=[lpar[:, 2]])
        DMA("sp", cwt, cw_d.ap(), writes=[cwt])
        DMA("sp", cbt, cb_d.ap(), writes=[cbt])
        ACTV(lpar[:, 2], lpar[:, 2], AF.Exp, scale=-1.0)
        ACTV(lpar[:, 2], lpar[:, 2], AF.Ln, bias=cvals[:, 1:2])
        TS("dve", lpar[:, 2], lpar[:, 2], -8.0, ALU.mult)
        for ti, (c0, T, col) in enumerate(TILES):
            if col is None:
                col = seq
            if ti == 1:
                drain()
            modulate(xfull, l, 1, c0, T, col, dcol0=c0)
        lsrc = s_lin.ap().rearrange("(kc p) n -> p kc n", p=128)
        bk = {"n": 0}

        def nbank():
            bk["n"] += 1
            return pb[5 + bk["n"] % 3]

        def front_a(c):
            U1, U2, ubf = UU[c % 2]
            ws = wsl[c % 2]
            wx_ = wax[c % 2]
            DMA("sp", ws[:, :, 0, :], lsrc[:, :, c * 128:(c + 1) * 128], reads=["s_lin"], writes=[ws[:, :, 0, :]], key=("wsl", c % 2, 0))
            DMA("sp", ws[:, :, 1, :], lsrc[:, :, D + c * 128:D + (c + 1) * 128], reads=["s_lin"], writes=[ws[:, :, 1, :]], key=("wsl", c % 2, 1))
            DMA("sp", wx_[:, 0:4:2, :], s_wa.ap()[:, c].rearrange("d i j -> i d j"), reads=["s_wa"], writes=[wx_[:, 0:4:2, :]], key=("wax", c % 2, 0))
            DMA("sp", wx_[:, 1:4:2, :], s_wx.ap()[:, c].rearrange("d i j -> i d j"), reads=["s_wx"], writes=[wx_[:, 1:4:2, :]], key=("wax", c % 2, 1))
            MEMSET("dve", U1[:, 0:1], 0.0)
            MEMSET("dve", U1[:, 257:260], 0.0)
            MEMSET("dve", U1[:, 2308:2310], 0.0)
            for ti, (c0, T, col) in enumerate(TILES):
                bank = pb[ti]
                for kc in range(NCH):
                    MM(bank[:, :T], ws[:, kc, 1, :], xfull[:, kc, c0:c0 + T], kc == 0, kc == NCH - 1)
            for ti, (c0, T, col) in enumerate(TILES):
                uo = CO + c0 if ti == 0 else LO + (c0 - CTX)
                CP("act", U1[:, uo:uo + T], pb[ti][:, :T])
            for (uo, c0, T) in ((CO, 0, CTX), (LO, CTX, SEQ)):
                ACTV(U2[:, c0:c0 + T], U1[:, uo - 1:uo - 1 + T], AF.Identity, scale=cwt[:, c, 0:1], bias=cbt[:, c:c + 1])

        def front_b(c):
            U1, U2, ubf = UU[c % 2]
            for (uo, c0, T) in ((CO, 0, CTX), (LO, CTX, SEQ)):
                for tap in range(1, 4):
                    STT("dve", U2[:, c0:c0 + T], U1[:, uo - 1 + tap:uo - 1 + tap + T], cwt[:, c, tap:tap + 1], U2[:, c0:c0 + T], ALU.mult, ALU.add)

        def front_c(c):
            U1, U2, ubf = UU[c % 2]
            for (c0_, T_, _c) in TILES:
                CP("act", ubf[:, c0_:c0_ + T_], U2[:, c0_:c0_ + T_])

        def direction(c, dr):
            U1, U2, ubf = UU[c % 2]
            wx_ = wax[c % 2]
            T_ = (U1 if dr == 0 else U2)[:, 0:L]
            for ti, (c0, T, col) in enumerate(TILES):
                bank = nbank()
                MM(bank[:, :T], wx_[:, 2 * dr, :], ubf[:, c0:c0 + T], True, True)
                ACTV(LA[:, c0:c0 + T], bank[:, :T], AF.Sigmoid, bias=lpar[:, 0, dr, c:c + 1])
                bank2 = nbank()
                MM(bank2[:, :T], wx_[:, 2 * dr + 1, :], ubf[:, c0:c0 + T], True, True)
                ACTV(LBb[:, c0:c0 + T], bank2[:, :T], AF.Sigmoid, bias=lpar[:, 1, dr, c:c + 1])
            ACTV(LA, LA, AF.Exp, scale=lpar[:, 2, dr, c:c + 1])
            TT("dve", LBb, LBb, U2, ALU.mult)
            ACTV(T_, LA, AF.Square)
            ACTV(T_, T_, AF.Sqrt, scale=-1.0, bias=cvals[:, 1:2])
            TT("dve", LBb, LBb, T_, ALU.mult)
            so = T_
            if dr == 0:
                S.op("dve", lambda e: e.tensor_tensor_scan(out=so[:, 0:CTX], data0=LA[:, 0:CTX], data1=LBb[:, 0:CTX], initial=0.0, op0=ALU.mult, op1=ALU.add),
                     reads=[LA[:, 0:CTX], LBb[:, 0:CTX]], writes=[so[:, 0:CTX]])
                S.op("dve", lambda e: e.tensor_tensor_scan(out=so[:, CTX:L], data0=LA[:, CTX:L], data1=LBb[:, CTX:L], initial=so[:, CTX - 1:CTX], op0=ALU.mult, op1=ALU.add),
                     reads=[LA[:, CTX:L], LBb[:, CTX:L], so[:, CTX - 1:CTX]], writes=[so[:, CTX:L]])
            else:
                S.op("dve", lambda e: e.tensor_tensor_scan(out=so[:, CTX - 1::-1], data0=LA[:, CTX - 1::-1], data1=LBb[:, CTX - 1::-1], initial=0.0, op0=ALU.mult, op1=ALU.add),
                     reads=[LA[:, 0:CTX], LBb[:, 0:CTX]], writes=[so[:, 0:CTX]])
                S.op("dve", lambda e: e.tensor_tensor_scan(out=so[:, L - 1:CTX - 1:-1], data0=LA[:, L - 1:CTX - 1:-1], data1=LBb[:, L - 1:CTX - 1:-1], initial=so[:, 0:1], op0=ALU.mult, op1=ALU.add),
                     reads=[LA[:, CTX:L], LBb[:, CTX:L], so[:, 0:1]], writes=[so[:, CTX:L]])

        def tail(c):
            U1, U2, ubf = UU[c % 2]
            ws = wsl[c % 2]
            cg = c % 4
            TT("dve", LA[:, CTX:L], U1[:, CTX:L], U2[:, CTX:L], ALU.add)
            for ti in range(1, 5):
                c0, T, _ = TILES[ti]
                bank = pb[ti - 1]
                for kc in range(NCH):
                    MM(bank[:, :T], ws[:, kc, 0, :], xfull[:, kc, c0:c0 + T], kc == 0, kc == NCH - 1)
            for ti in range(1, 5):
                c0, T, _ = TILES[ti]
                ACTV(LBb[:, c0:c0 + T], pb[ti - 1][:, :T], AF.Gelu_apprx_tanh)
                TT("dve", gy[:, cg, c0 - CTX:c0 - CTX + T], LBb[:, c0:c0 + T], LA[:, c0:c0 + T], ALU.mult)

        def outproj(grp):
            DMA("sp", w_loh, s_lo.ap()[grp * 512:(grp + 1) * 512, :].rearrange("(c p) n -> p c n", p=128), reads=["s_lo"], writes=[w_loh])
            for ti in range(1, 5):
                c0, T, _ = TILES[ti]
                for ch in range(NCH):
                    yps = pb[ch % 2]
                    for c4 in range(4):
                        MM(yps[:, :T], w_loh[:, c4, ch * 128:(ch + 1) * 128], gy[:, c4, c0 - CTX:c0 - CTX + T], c4 == 0, c4 == 3)
                    if grp == 0:
                        hs = h[:, ch, c0:c0 + T]
                        STT("dve", hs, yps[:, :T], mvec(l, 5, ch, seq), hs, ALU.mult, ALU.add)
                    else:
                        resid(ch, yps[:, :T], l, 1, c0, T, seq)
                        drain(1)
                if grp == 1:
                    drain()
                    layernorm(l, 1, c0, T)

        front_a(0)
        front_b(0)
        front_c(0)
        for c in range(NCH):
            nxt = c + 1 < NCH
            if nxt:
                front_a(c + 1)
                front_b(c + 1)
            direction(c, 0)
            if nxt:
                front_c(c + 1)
            direction(c, 1)
            tail(c)
            if c % 4 == 3:
                outproj(c // 4)

    for seq in range(SPC):
        cnt["dma"] = 1000
        load_seq(seq)
        stage = 0

        def done():
            return stop_after is not None and stage >= stop_after
        if seq == 0:
            castq.append(("ab", cast_ab))
            cast_ffn(0, 1, "f01")
            cast_ffn(1, 0, "f10")
            castq.append(("lru", cast_lru))
            cast_ffn(1, 1, "f11")
        ffn(0, 0, seq, TILES); stage = 1
        if not done():
            flush_tag("ab")
            mixer_ab(seq); stage = 2
        if not done():
            flush_tag("f01")
            ffn(0, 1, seq, TILES); stage = 3
        if not done():
            flush_tag("f10")
            ffn(1, 0, seq, TILES); stage = 4
        if not done():
            flush_tag("lru")
            mixer_lru(seq); stage = 5
        if not done():
            flush_tag("f11")
            ffn(1, 1, seq, TILES[1:]); stage = 6
        drain()
        store_seq(seq)

    S.emit(final_dma_waits=out_dmas)
    st.close()
    return nc, S


_CACHE = {}


def _prep_inputs(inputs, ncores=NCORES):
    f = lambda a: np.ascontiguousarray(np.asarray(a, dtype=np.float32))
    x = f(inputs["x"]); c = f(inputs["c"]); ctx = f(inputs["ctx"]); c_ctx = f(inputs["c_ctx"])
    shared = {k: f(inputs[k]) for k in ("w_mod", "ffn_w_gate", "ffn_w_up", "ffn_w_down", "mix_ab_w_in", "attn_sink", "pool_w",
                                         "mix_ab_w_out", "lru_w_in", "lru_wa", "lru_wx", "lru_w_out")}
    b_mod = f(inputs["b_mod"])
    bm = b_mod.reshape(2, 72, 128).transpose(0, 2, 1)
    shared["b_mod3"] = np.ascontiguousarray(np.repeat(bm[:, :, :, None], 3, axis=3))
    pl = lambda a: np.ascontiguousarray(a)
    shared["ln_g"] = pl(f(inputs["ln_g"]).reshape(2, 3, NCH, 128).transpose(3, 0, 1, 2))
    shared["ln_b"] = pl(f(inputs["ln_b"]).reshape(2, 3, NCH, 128).transpose(3, 0, 1, 2))
    shared["pool_scale"] = pl(f(inputs["pool_scale"]).reshape(4, 128).T)
    shared["lru_conv_w"] = pl(f(inputs["lru_conv_w"]).reshape(4, NCH, 128).transpose(2, 1, 0))
    shared["lru_conv_b"] = pl(f(inputs["lru_conv_b"]).reshape(NCH, 128).T)
    for k in ("lru_ba", "lru_bx", "lru_lambda"):
        shared[k] = pl(f(inputs[k]).reshape(2, NCH, 128).transpose(2, 0, 1))
    shared.update(_host_consts())
    in_maps = []
    for i in range(ncores):
        m = dict(shared)
        m["x"] = np.ascontiguousarray(x[SPC * i:SPC * (i + 1)])
        m["ctx"] = np.ascontiguousarray(ctx[SPC * i:SPC * (i + 1)])
        cv = np.stack([c[SPC * i], c[SPC * i + 1], c_ctx], axis=-1)
        m["cvec"] = np.ascontiguousarray(cv.reshape(NCH, 128, 3).transpose(1, 0, 2))
        in_maps.append(m)
    return in_maps


def kernel(**inputs):
    if "nc" not in _CACHE:
        _CACHE["nc"] = build()[0]
    nc = _CACHE["nc"]
    in_maps = _prep_inputs(inputs)
    res = run_bass_kernel_spmd(nc, in_maps, core_ids=list(range(NCORES)))
    return np.concatenate([r["out"] for r in res.results], axis=0)
```
